# Optimizing a Trainium2 kernel written in Bass

```python
import jax, jax.numpy as jnp
from jax import lax
import numpy as np

D_MODEL = 1024
BATCH = 8
SEQ = 2048
DEPTH = 1

PLE_DIM = 256
CONV_WIDTH = D_MODEL // 2
CONV_GROUPS = 8
CONV_K = 3
RET_WIDTH = D_MODEL - CONV_WIDTH
RET_HEADS = 4
RET_HEAD_DIM = RET_WIDTH // RET_HEADS
D_FF = 4 * D_MODEL
CHUNK = 128
ROPE_BASE = 10000.0
LN_EPS = 1e-5
GN_EPS = 1e-5
DEEPNORM_ALPHA = (2.0 * DEPTH) ** 0.25
DEEPNORM_BETA = (8.0 * DEPTH) ** -0.25
IN_COLS = 3 * CONV_WIDTH + 4 * RET_WIDTH

kernel_name = "hybrid_conv_retention_deepnorm_layer"


def layer_norm(x, w, b):
    xf = x.astype(jnp.float32)
    mu = jnp.mean(xf, axis=-1, keepdims=True)
    var = jnp.mean(jnp.square(xf - mu), axis=-1, keepdims=True)
    y = (xf - mu) * lax.rsqrt(var + LN_EPS)
    return (y * w.astype(jnp.float32) + b.astype(jnp.float32)).astype(x.dtype)


def rotary(x, positions):
    dh = x.shape[-1]
    half = dh // 2
    inv_freq = ROPE_BASE ** (-jnp.arange(half, dtype=jnp.float32) * 2.0 / dh)
    ang = positions.astype(jnp.float32)[:, :, None] * inv_freq
    cos = jnp.cos(ang)[:, :, None, :]
    sin = jnp.sin(ang)[:, :, None, :]
    xf = x.astype(jnp.float32)
    x1, x2 = xf[..., :half], xf[..., half:]
    out = jnp.concatenate([x1 * cos - x2 * sin, x2 * cos + x1 * sin], axis=-1)
    return out.astype(x.dtype)


def causal_short_conv(u, w):
    s = u.shape[1]
    up = jnp.pad(u, ((0, 0), (CONV_K - 1, 0), (0, 0)))
    return sum(up[:, k:k + s, :] * w[k] for k in range(CONV_K))


def chunkwise_retention(q, k, v):
    bsz, s, h, d = q.shape
    n = s // CHUNK
    dt = q.dtype
    qf = q.astype(jnp.float32).reshape(bsz, n, CHUNK, h, d).transpose(0, 3, 1, 2, 4)
    kf = k.astype(jnp.float32).reshape(bsz, n, CHUNK, h, d).transpose(0, 3, 1, 2, 4)
    vf = v.astype(jnp.float32).reshape(bsz, n, CHUNK, h, d).transpose(0, 3, 1, 2, 4)
    log_g = jnp.log1p(-jnp.exp2(-5.0 - jnp.arange(h, dtype=jnp.float32)))
    idx = jnp.arange(CHUNK, dtype=jnp.float32)
    diff = idx[:, None] - idx[None, :]
    causal = diff >= 0
    decay_mat = jnp.where(causal[None], jnp.exp(jnp.where(causal, diff, 0.0)[None] * log_g[:, None, None]), 0.0)
    scores = jnp.einsum('bhnid,bhnjd->bhnij', qf, kf) * decay_mat[None, :, None]
    inner = jnp.einsum('bhnij,bhnje->bhnie', scores, vf)
    zeta = jnp.exp((CHUNK - 1.0 - idx)[None, :] * log_g[:, None])
    kv = jnp.einsum('bhncd,hc,bhnce->bhnde', kf, zeta, vf)
    chunk_decay = jnp.exp(CHUNK * log_g)[None, :, None, None]

    def step(state, kv_n):
        return state * chunk_decay + kv_n, state

    init = jnp.zeros((bsz, h, d, d), jnp.float32)
    _, prev_states = lax.scan(step, init, jnp.moveaxis(kv, 2, 0))
    prev_states = jnp.moveaxis(prev_states, 0, 2)
    xi = jnp.exp((idx + 1.0)[None, :] * log_g[:, None])
    cross = jnp.einsum('bhncd,hc,bhnde->bhnce', qf, xi, prev_states)
    out = (inner + cross).transpose(0, 2, 3, 1, 4).reshape(bsz, s, h, d)
    return out.astype(dt)


def head_group_norm(o, w, b):
    of = o.astype(jnp.float32)
    mu = jnp.mean(of, axis=-1, keepdims=True)
    var = jnp.mean(jnp.square(of - mu), axis=-1, keepdims=True)
    y = (of - mu) * lax.rsqrt(var + GN_EPS)
    y = y.reshape(o.shape[0], o.shape[1], -1)
    return (y * w.astype(jnp.float32) + b.astype(jnp.float32)).astype(o.dtype)


def setup_inputs(seed: int = 0) -> dict:
    key = jax.random.key(seed)
    ks = jax.random.split(key, 16)
    f32 = jnp.float32
    nrm = lambda k, shape, scale: jax.random.normal(k, shape, f32) * scale
    x = jax.random.normal(ks[0], (BATCH, SEQ, D_MODEL), f32)
    p = jax.random.normal(ks[1], (DEPTH, BATCH, SEQ, PLE_DIM), f32)
    positions = jnp.broadcast_to(jnp.arange(SEQ, dtype=jnp.int32)[None, :], (BATCH, SEQ))
    w_in = nrm(ks[2], (DEPTH, D_MODEL, IN_COLS), D_MODEL ** -0.5)
    conv_w = nrm(ks[3], (DEPTH, CONV_K, CONV_WIDTH), CONV_K ** -0.5)
    ret_gn_w = 1.0 + nrm(ks[4], (DEPTH, RET_WIDTH), 0.02)
    ret_gn_b = nrm(ks[5], (DEPTH, RET_WIDTH), 0.02)
    w_out = nrm(ks[6], (DEPTH, D_MODEL, D_MODEL), D_MODEL ** -0.5 * DEEPNORM_BETA)
    ln1_w = 1.0 + nrm(ks[7], (DEPTH, D_MODEL), 0.02)
    ln1_b = nrm(ks[8], (DEPTH, D_MODEL), 0.02)
    w_ff1 = nrm(ks[9], (DEPTH, D_MODEL, D_FF), D_MODEL ** -0.5)
    w_ff2 = nrm(ks[10], (DEPTH, D_FF, D_MODEL), D_FF ** -0.5 * DEEPNORM_BETA)
    w_ple_gate = nrm(ks[11], (DEPTH, D_MODEL, D_MODEL), D_MODEL ** -0.5)
    w_ple_proj = nrm(ks[12], (DEPTH, PLE_DIM, D_MODEL), PLE_DIM ** -0.5 * DEEPNORM_BETA)
    ln2_w = 1.0 + nrm(ks[13], (DEPTH, D_MODEL), 0.02)
    ln2_b = nrm(ks[14], (DEPTH, D_MODEL), 0.02)
    return {"x": x, "p": p, "positions": positions, "w_in": w_in, "conv_w": conv_w,
            "ret_gn_w": ret_gn_w, "ret_gn_b": ret_gn_b, "w_out": w_out,
            "ln1_w": ln1_w, "ln1_b": ln1_b, "w_ff1": w_ff1, "w_ff2": w_ff2,
            "w_ple_gate": w_ple_gate, "w_ple_proj": w_ple_proj,
            "ln2_w": ln2_w, "ln2_b": ln2_b}


def reference(x, p, positions, w_in, conv_w, ret_gn_w, ret_gn_b, w_out, ln1_w, ln1_b,
              w_ff1, w_ff2, w_ple_gate, w_ple_proj, ln2_w, ln2_b):
    bsz, s, _ = x.shape
    cw, rw = CONV_WIDTH, RET_WIDTH
    split_at = [cw, 2 * cw, 3 * cw, 3 * cw + rw, 3 * cw + 2 * rw, 3 * cw + 3 * rw]
    for i in range(DEPTH):
        hcat = x @ w_in[i]
        b_gate, c_gate, hc, q, k, v, g = jnp.split(hcat, split_at, axis=-1)
        y_conv = b_gate * causal_short_conv(c_gate * hc, conv_w[i])
        q = rotary(q.reshape(bsz, s, RET_HEADS, RET_HEAD_DIM), positions)
        k = rotary(k.reshape(bsz, s, RET_HEADS, RET_HEAD_DIM), positions) * (RET_HEAD_DIM ** -0.5)
        v = v.reshape(bsz, s, RET_HEADS, RET_HEAD_DIM)
        o = chunkwise_retention(q, k, v)
        y_ret = jax.nn.silu(g) * head_group_norm(o, ret_gn_w[i], ret_gn_b[i])
        mix = jnp.concatenate([y_conv, y_ret], axis=-1) @ w_out[i]
        x = layer_norm(DEEPNORM_ALPHA * x + mix, ln1_w[i], ln1_b[i])
        ff = jnp.square(jax.nn.relu(x @ w_ff1[i])) @ w_ff2[i]
        ple = jax.nn.sigmoid(x @ w_ple_gate[i]) * (p[i] @ w_ple_proj[i])
        x = layer_norm(DEEPNORM_ALPHA * x + ff + ple, ln2_w[i], ln2_b[i])
    return x
```

```python
from contextlib import ExitStack
import numpy as np
import concourse.bass as bass
import concourse.mybir as mybir
from concourse.bass_utils import run_bass_kernel_spmd

F32 = mybir.dt.float32
BF16 = mybir.dt.bfloat16
I32 = mybir.dt.int32
AF = mybir.ActivationFunctionType
ALU = mybir.AluOpType
AX = mybir.AxisListType

ENGS = ("pe", "act", "dve", "pool", "sp")
STRICT_SYNC = True

S = 2048
D = 1024
NB = 16
ALPHA = float(2.0 ** 0.25)
LN_EPS = 1e-5
GN_EPS = 1e-5
PI = float(np.pi)


class _Op:
    __slots__ = ("eng", "fn", "deps", "is_dma", "sem_key", "signals", "sig_val", "idx")


def _excl(r):
    return isinstance(r, tuple) and r[0] == "ps"


class Prog:
    def __init__(self, nc):
        self.nc = nc
        self.ops = []
        self.last_writer = {}
        self.readers = {}
        self.written = set()
        self.alias = {}
        self._alias_seen = set()

    def _add(self, eng, fn, reads, writes, is_dma=False, sem_key=None):
        extra = []
        for r in list(reads) + list(writes):
            if r in self.alias and r not in self._alias_seen:
                self._alias_seen.add(r)
                pa = self.alias[r]
                extra.extend(pa if isinstance(pa, list) else [pa])
        if extra:
            writes = list(writes) + extra
        o = _Op()
        o.eng, o.fn, o.is_dma, o.sem_key = eng, fn, is_dma, sem_key
        o.signals = is_dma
        o.sig_val = 0
        o.idx = len(self.ops)
        deps = {}
        for r in reads:
            w = self.last_writer.get(r)
            if w is not None:
                deps[w] = True
            if _excl(r):
                rd = self.readers.get(r)
                if rd:
                    for x in rd.values():
                        deps.setdefault(x, False)
        for r in writes:
            w = self.last_writer.get(r)
            if w is not None:
                deps.setdefault(w, False)
            rd = self.readers.get(r)
            if rd:
                for x in rd.values():
                    deps.setdefault(x, False)
        o.deps = deps
        for r in reads:
            d = self.readers.setdefault(r, {})
            key = ("dma", o.idx) if is_dma else eng
            d[key] = o.idx
        for r in writes:
            self.last_writer[r] = o.idx
            self.readers[r] = {}
            self.written.add(r)
        self.ops.append(o)
        return o

    def op(self, eng, fn, reads=(), writes=()):
        return self._add(eng, fn, reads, writes)

    def dma(self, eng, fn, sem_key, reads=(), writes=()):
        return self._add(eng, fn, reads, writes, is_dma=True, sem_key=sem_key)

    def wait_all(self, eng, resources):
        return self._add(eng, None, list(resources), [])

    def join_all(self):
        res = list(self.written)
        last = {}
        for o in self.ops:
            if o.fn is not None:
                last[o.eng] = o.idx
        for e in ENGS:
            o = self._add(e, None, res, [])
            for le, li in last.items():
                if le != e:
                    o.deps.setdefault(li, True)

    def emit(self, stack, tag, sem_stack=None):
        nc = self.nc
        sem_stack = sem_stack or stack
        ops = self.ops
        for o in ops:
            nd = set()
            for d, raw in o.deps.items():
                p = ops[d]
                if p.fn is None or d == o.idx:
                    continue
                if (not p.is_dma) and p.eng == o.eng:
                    if o.eng == "pe" or not (raw or STRICT_SYNC):
                        continue
                nd.add(d)
            o.deps = nd
            for d in nd:
                ops[d].signals = True
        esem = {e: sem_stack.enter_context(nc.semaphore("sem_%s_%s" % (tag, e))) for e in ENGS}
        dsem = {}
        ecount = {e: 0 for e in ENGS}
        dcount = {}
        for o in ops:
            if o.is_dma:
                if o.sem_key not in dsem:
                    dsem[o.sem_key] = sem_stack.enter_context(
                        nc.semaphore("dsem_%s_%d" % (tag, len(dsem))))
                    dcount[o.sem_key] = 0
                dcount[o.sem_key] += 16
                o.sig_val = dcount[o.sem_key]
            elif o.signals and o.fn is not None:
                ecount[o.eng] += 1
                o.sig_val = ecount[o.eng]

        def sem_of(p):
            return dsem[p.sem_key] if p.is_dma else esem[p.eng]

        block = stack.enter_context(nc.Block())

        def run_engine(ename):
            def body(e):
                waited = {}
                for o in ops:
                    if o.eng != ename:
                        continue
                    need = {}
                    for d in o.deps:
                        p = ops[d]
                        s = sem_of(p)
                        k = id(s)
                        if k not in need or need[k][1] < p.sig_val:
                            need[k] = (s, p.sig_val)
                    for k, (s, v) in need.items():
                        if waited.get(k, 0) >= v:
                            continue
                        e.wait_ge(s, v)
                        waited[k] = v
                    if o.fn is None:
                        continue
                    inst = o.fn(e)
                    if o.is_dma:
                        inst.then_inc(dsem[o.sem_key], 16)
                    elif o.signals:
                        inst.then_inc(esem[ename], 1)
            return body

        block.tensor(run_engine("pe"))
        block.scalar(run_engine("act"))
        block.vector(run_engine("dve"))
        block.gpsimd(run_engine("pool"))
        block.sync(run_engine("sp"))


class H:
    def __init__(self, P):
        self.P = P

    def mm(self, out, lhsT, rhs, start, stop, reads, writes):
        self.P.op("pe", lambda e: e.matmul(out, lhsT=lhsT, rhs=rhs, start=start, stop=stop),
                  reads, writes)

    def tr(self, out, in_, ident, reads, writes):
        self.P.op("pe", lambda e: e.transpose(out=out, in_=in_, identity=ident), reads, writes)

    def tt(self, eng, out, in0, in1, op, reads, writes):
        self.P.op(eng, lambda e: e.tensor_tensor(out=out, in0=in0, in1=in1, op=op), reads, writes)

    def ts(self, eng, out, in0, s1, op0, reads, writes, s2=None, op1=None):
        if op1 is None:
            self.P.op(eng, lambda e: e.tensor_scalar(out=out, in0=in0, scalar1=s1, scalar2=None,
                                                     op0=op0), reads, writes)
        else:
            self.P.op(eng, lambda e: e.tensor_scalar(out=out, in0=in0, scalar1=s1, scalar2=s2,
                                                     op0=op0, op1=op1), reads, writes)

    def stt(self, out, in0, scalar, in1, op0, op1, reads, writes):
        self.P.op("dve", lambda e: e.scalar_tensor_tensor(out=out, in0=in0, scalar=scalar, in1=in1,
                                                          op0=op0, op1=op1), reads, writes)

    def act(self, out, in_, func, reads, writes, scale=None, bias=None):
        kw = {}
        if scale is not None:
            kw["scale"] = scale
        if bias is not None:
            kw["bias"] = bias
        self.P.op("act", lambda e: e.activation(out=out, in_=in_, func=func, **kw), reads, writes)

    def cp(self, eng, out, in_, reads, writes):
        if eng == "act":
            self.act(out, in_, AF.Copy, reads, writes)
        else:
            self.P.op(eng, lambda e: e.tensor_copy(out=out, in_=in_), reads, writes)

    def red(self, out, in_, reads, writes):
        self.P.op("dve", lambda e: e.tensor_reduce(out=out, in_=in_, op=ALU.add, axis=AX.X),
                  reads, writes)

    def dma(self, eng, out, in_, key, reads, writes, **kw):
        self.P.dma(eng, lambda e: e.dma_start(out=out, in_=in_, **kw), key, reads, writes)


def make_gen_ln(P, h, lnst, neghalf):
    def gen_ln(buf, rbuf, w_ap, b_ap, rw, rb, slot):
        st6 = lnst[:, slot, 0:12]
        mv = lnst[:, slot, 12:14]
        rstd = lnst[:, slot, 14:15]
        nbias = lnst[:, slot, 15:16]
        rl = ("lnst", slot)
        for hf in range(2):
            P.op("dve", lambda e, hf=hf: e.bn_stats(out=st6[:, hf * 6:(hf + 1) * 6],
                                                    in_=buf[:, hf * 512:(hf + 1) * 512]),
                 [rbuf], [rl])
            yield
        P.op("dve", lambda e: e.bn_aggr(out=mv, in_=st6), [rl], [rl])
        yield
        h.ts("dve", rstd, mv[:, 1:2], LN_EPS, ALU.add, [rl], [rl])
        yield
        h.tt("pool", rstd, rstd, neghalf[:, 0:1], ALU.pow, [rl, "neghalf"], [rl])
        yield
        yield
        h.stt(nbias, mv[:, 0:1], -1.0, rstd, ALU.mult, ALU.mult, [rl], [rl])
        yield
        h.act(buf, buf, AF.Identity, [rbuf, rl], [rbuf], scale=rstd, bias=nbias)
        yield
        yield
        h.tt("dve", buf, buf, w_ap, ALU.mult, [rbuf, rw], [rbuf])
        yield
        h.tt("dve", buf, buf, b_ap, ALU.add, [rbuf, rb], [rbuf])
        yield
    return gen_ln


def build_nc():
    nc = bass.Bass("TRN2", target_bir_lowering=False)

    def din(name, shape, dt=F32):
        return nc.dram_tensor(name, shape, dt, kind="ExternalInput").ap()

    x = din("x", [S, D])
    p_in = din("p", [S, 256])
    pos = din("pos", [S], I32)
    w_in = din("w_in", [D, 3584])
    conv_w = din("conv_w", [3, 512])
    gn_w = din("gn_w", [512])
    gn_b = din("gn_b", [512])
    w_out = din("w_out", [D, D])
    ln1_w = din("ln1_w", [D])
    ln1_b = din("ln1_b", [D])
    w_ff1 = din("w_ff1", [D, 4096])
    w_ff2 = din("w_ff2", [4096, D])
    w_gate = din("w_gate", [D, D])
    w_proj = din("w_proj", [256, D])
    ln2_w = din("ln2_w", [D])
    ln2_b = din("ln2_b", [D])
    c_invf = din("c_invf", [64])
    c_xi = din("c_xi", [512])
    c_mask = din("c_mask", [128, 512])
    c_zeta = din("c_zeta", [128, 4])
    c_cd = din("c_cd", [512])
    out = nc.dram_tensor("out", [S, D], F32, kind="ExternalOutput").ap()

    s_ff1 = nc.dram_tensor("s_ff1", [8, 128, 8, 512], BF16).ap()
    s_ff2 = nc.dram_tensor("s_ff2", [8, 128, 32, 128], BF16).ap()
    s_wout = nc.dram_tensor("s_wout", [2, 128, 8, 512], BF16).ap()
    s_gate = nc.dram_tensor("s_gate", [2, 128, 8, 512], BF16).ap()

    with ExitStack() as st0:
        def sb0(name, shape, dt):
            return st0.enter_context(nc.sbuf_tensor(name, shape, dt))

        ycat = sb0("ycat", [128, 8, S], BF16)
        ident = sb0("ident", [128, 128], BF16)
        identf = sb0("identf", [128, 128], F32)
        neghalf = sb0("neghalf", [128, 4], F32)
        ubuf = [sb0("ubuf%d" % i, [128, 2 + S], F32) for i in range(2)]

        def x1v(bi):
            return ubuf[bi // 2][:, (bi % 2) * 1024:(bi % 2 + 1) * 1024]
        kpre = sb0("kpre", [128, 8, 512], BF16)

        with ExitStack() as st:
            def sb(name, shape, dt):
                return st.enter_context(nc.sbuf_tensor(name, shape, dt))

            def psf(name):
                return st.enter_context(nc.psum_tensor(name, [128, 512], F32))

            def psb(name):
                return st.enter_context(nc.psum_tensor(name, [128, 8, 128], BF16))

            P = Prog(nc)
            h = H(P)

            xT = sb("xT", [128, 8, S], BF16)
            wsl = [sb("wsl%d" % i, [128, 8, 512], BF16) for i in range(7)]
            wB, wC, wH, wQ, wK, wV, wG = wsl
            NXB = 4
            xb = [ycat[:, 4 + i, :].bitcast(F32) for i in range(NXB)]
            for tt_ in range(4):
                P.alias[("ycat", tt_)] = [("xb", i) for i in range(NXB)]
            rows = sb("rows", [32, 128], F32)
            posi = sb("posi", [16, 128], I32)
            posf16 = sb("posf16", [16, 128], F32)
            smallT = sb("smallT", [128, 20], F32)
            posT = sb("posT", [128, 16], F32)
            invf = sb("invf", [128, 64], F32)
            cosF = sb("cosF", [128, 16, 128], F32)
            sinS = sb("sinS", [128, 16, 128], F32)
            xiT = sb("xiT", [128, 4, 128], F32)
            maskT = sb("maskT", [128, 4, 128], F32)
            zeta = sb("zeta", [128, 4], F32)
            cdT = sb("cdT", [128, 512], F32)
            hs = [sb("hs%d" % i, [128, 512], F32) for i in range(2)]
            abuf = [sb("abuf%d" % i, [128, 512], F32) for i in range(2)]
            ang = xb[0].rearrange("p (a c) -> p a c", c=64)
            kkf = xb[1].rearrange("p (a c) -> p a c", c=64)
            kki_t = sb("kki_t", [128, 1024], I32)
            kki = kki_t[:].rearrange("p (a c) -> p a c", c=64)
            R_ANG = ("xb", 0)
            R_KKF = ("xb", 1)
            R_KKI = "kki_t"
            def f4(ap):
                return ap.rearrange("p (h d) -> p h d", h=4)
            def wf(i):
                return wsl[i][:].rearrange("p a b -> p (a b)").bitcast(F32)
            t1 = [f4(wf(0)[:, i * 1024:i * 1024 + 512]) for i in range(2)]
            t2 = [f4(wf(0)[:, i * 1024 + 512:i * 1024 + 1024]) for i in range(2)]
            gsr = [f4(wf(1)[:, i * 512:(i + 1) * 512]) for i in range(2)]
            yn = [f4(wf(1)[:, 1024 + i * 512:1024 + (i + 1) * 512]) for i in range(2)]
            Sst = hs[0]
            Stmp = hs[1]
            sq = f4(abuf[0][:])
            oc = f4(abuf[1][:])
            for i in range(2):
                P.alias[("t1", i)] = [("wslj", 0, j) for j in range(4)]
                P.alias[("t2", i)] = [("wslj", 0, j) for j in range(4)]
                P.alias[("gsr", i)] = [("wslj", 1, j) for j in range(4)]
                P.alias[("yn", i)] = [("wslj", 1, j) for j in range(4)]
            lnstA = sb("lnstA", [128, 4, 16], F32)
            xinA = [wf(2)[:, i * 1024:(i + 1) * 1024] for i in range(2)]
            lnA = wf(5)
            x1bA_all = cosF[:].rearrange("p a b -> p (a b)").bitcast(BF16)
            x1bA = [x1bA_all[:, i * 1024:(i + 1) * 1024] for i in range(4)]
            for i in range(2):
                P.alias[("xinA", i)] = [("wslj", 2, j) for j in range(4)]
            P.alias["ln1wA"] = ("wsl", 5)
            P.alias["ln1bA"] = ("wsl", 5)
            for i in range(4):
                P.alias[("x1bA", i)] = "cosF"
                P.alias[("x1", i)] = [("ubuf", i // 2), ("U", 2 * (i // 2)), ("U", 2 * (i // 2) + 1)]
                P.alias[("U", i)] = ("ubuf", i // 2)
            P.alias["Sst"] = ("hs", 0)
            P.alias["Stmp"] = ("hs", 1)
            P.alias["sq"] = ("ab", 0)
            P.alias["oc"] = ("ab", 1)
            qr = [sb("qr%d" % i, [128, 512], BF16) for i in range(2)]
            kr = [sb("kr%d" % i, [128, 512], BF16) for i in range(4)]
            vb = [sb("vb%d" % i, [128, 512], BF16) for i in range(4)]
            vz = [sb("vz%d" % i, [128, 512], BF16) for i in range(4)]
            qhT = [sb("qhT%d" % i, [128, 4, 128], BF16) for i in range(3)]
            kT = [sb("kT%d" % i, [128, 4, 128], BF16) for i in range(2)]
            PT = [sb("PT%d" % i, [128, 4, 128], BF16) for i in range(2)]
            Sb = [sb("Sb%d" % i, [128, 512], BF16) for i in range(2)]
            ob = [sb("ob%d" % i, [128, 512], BF16) for i in range(2)]
            st4 = sb("st4", [128, 2, 28], F32)

            pf = [psf("pf%d" % i) for i in range(6)]
            pb = [psb("pb%d" % i) for i in range(2)]
            PF = [("ps", i) for i in range(6)]
            PB = [("ps", 6 + i) for i in range(2)]

            P.op("pool", lambda e: e.memset(identf[:], 0.0), [], ["identf"])
            P.op("pool", lambda e: e.affine_select(
                out=identf[:], in_=identf[:], pattern=[[-1, 128]], compare_op=ALU.not_equal,
                fill=1.0, base=0, channel_multiplier=1), ["identf"], ["identf"])
            h.cp("dve", ident[:], identf[:], ["identf"], ["ident"])
            P.op("pool", lambda e: e.memset(neghalf[:], -0.5), [], ["neghalf"])
            P.op("pool", lambda e: e.memset(rows[:], 0.0), [], ["rows"])

            def load_x(blk):
                h.dma("sp", xb[blk % NXB], x[blk * 128:(blk + 1) * 128, :], ("xb", blk % NXB),
                      [], [("xb", blk % NXB)])

            def load_w(i):
                h.dma("pool", wsl[i][:],
                      w_in[:, i * 512:(i + 1) * 512].rearrange("(kc p) n -> p kc n", p=128),
                      ("wsl", i), [], [("wsl", i)])

            def load_wj(i, j):
                h.dma("pool", wsl[i][:, :, j * 128:(j + 1) * 128],
                      w_in[:, i * 512 + j * 128:i * 512 + (j + 1) * 128].rearrange(
                          "(kc p) n -> p kc n", p=128),
                      ("wslj", i, j), [], [("wslj", i, j)])

            load_x(0)
            load_x(1)
            load_x(2)
            load_x(3)
            for j in range(4):
                for i in range(3):
                    load_wj(i, j)
            h.dma("act", rows[0:12, :], conv_w.rearrange("k (j p) -> (k j) p", p=128), "rows",
                  ["rows"], ["rows"])
            h.dma("act", rows[12:16, :], gn_w.rearrange("(h p) -> h p", p=128), "rows",
                  ["rows"], ["rows"])
            h.dma("act", rows[16:20, :], gn_b.rearrange("(h p) -> h p", p=128), "rows",
                  ["rows"], ["rows"])
            h.dma("act", posi[:], pos.rearrange("(b p) -> b p", p=128), "posi", [], ["posi"])
            h.dma("act", invf[:], c_invf.partition_broadcast(128), "invf", [], ["invf"])
            h.dma("act", xiT[:], c_xi.rearrange("(h i) -> h i", h=4).partition_broadcast(128),
                  "xiT", [], ["xiT"])
            h.dma("act", maskT[:], c_mask.rearrange("p (h i) -> p h i", h=4), "maskT", [], ["maskT"])
            h.dma("act", zeta[:], c_zeta, "zeta", [], ["zeta"])
            h.dma("act", cdT[:], c_cd.partition_broadcast(128), "cdT", [], ["cdT"])


            def setup_small():
                h.tr(pf[0][:, 0:20], rows[0:20, :], identf[0:20, 0:20], ["rows", "identf"], [PF[0]])
                h.cp("dve", smallT[:], pf[0][:, 0:20], [PF[0]], ["smallT"])
                h.cp("dve", posf16[:], posi[:], ["posi"], ["posf16"])
                h.tr(pf[1][:, 0:16], posf16[:], identf[0:16, 0:16], ["posf16", "identf"], [PF[1]])
                h.cp("dve", posT[:], pf[1][:, 0:16], [PF[1]], ["posT"])

            cwT = smallT[:, 0:12]
            gnwT = smallT[:, 12:16]
            gnbT = smallT[:, 16:20]

            def setup_rotary():
                h.tt("dve", ang, posT[:].unsqueeze(2).broadcast_to([128, 16, 64]),
                     invf[:].unsqueeze(1).broadcast_to([128, 16, 64]), ALU.mult,
                     ["posT", "invf"], [R_ANG])
                yield

                def reduce_sin(src, srcname, dst_lo, dst_hi, lo_scale, hi_scale, dstname):
                    h.ts("dve", kki, src, float(1.0 / (2 * PI)), ALU.mult, [srcname], [R_KKI])
                    yield
                    h.cp("dve", kkf, kki, [R_KKI], [R_KKF])
                    yield
                    h.stt(kkf, kkf, float(-2 * PI), src, ALU.mult, ALU.add,
                          [R_KKF, srcname], [R_KKF])
                    yield
                    h.ts("dve", kkf, kkf, -PI, ALU.max, [R_KKF], [R_KKF], s2=PI, op1=ALU.min)
                    yield
                    h.act(dst_lo, kkf, AF.Sin, [R_KKF], [dstname], scale=lo_scale)
                    h.act(dst_hi, kkf, AF.Sin, [R_KKF], [dstname], scale=hi_scale)
                    yield

                yield from reduce_sin(ang, R_ANG, sinS[:, :, 0:64], sinS[:, :, 64:128], -1.0, 1.0,
                                      "sinS")
                h.ts("dve", ang, ang, PI / 2.0, ALU.add, [R_ANG], [R_ANG])
                yield
                yield from reduce_sin(ang, R_ANG, cosF[:, :, 0:64], cosF[:, :, 64:128], 1.0, 1.0,
                                      "cosF")


            def x_block(blk):
                r_xb = ("xb", blk % NXB)
                b0 = 2 * (blk % 2)
                for kc in range(8):
                    bank = pf[b0 + kc // 4]
                    h.tr(bank[:, (kc % 4) * 128:(kc % 4 + 1) * 128],
                         xb[blk % NXB][:, kc * 128:(kc + 1) * 128], identf[:],
                         [r_xb, "identf"], [PF[b0 + kc // 4]])
                h.cp("act", xT[:, 0:4, blk * 128:(blk + 1) * 128],
                     pf[b0][:].rearrange("p (a d) -> p a d", a=4), [PF[b0]], [("xT", blk // 4)])
                h.cp("dve", xT[:, 4:8, blk * 128:(blk + 1) * 128],
                     pf[b0 + 1][:].rearrange("p (a d) -> p a d", a=4), [PF[b0 + 1]],
                     [("xT", blk // 4)])
                if blk + NXB < NB:
                    load_x(blk + NXB)

            for blk in range(4):
                x_block(blk)
            setup_small()
            for blk in range(4, NB):
                x_block(blk)
            for i in range(3, 7):
                load_w(i)

            casts = []
            for g in range(2):
                casts.append((s_wout[g], w_out[:, g * 512:(g + 1) * 512].rearrange(
                    "(kc p) n -> p kc n", p=128), ("s_cast_wout", g), ("s_wout", g)))
            casts.append((s_ff1[0], w_ff1[:, 0:512].rearrange(
                "(kc p) n -> p kc n", p=128), "s_cast_ff1_0", ("s_ff1", 0)))
            for g in range(1, 8):
                casts.append((s_ff1[g], w_ff1[:, g * 512:(g + 1) * 512].rearrange(
                    "(kc p) n -> p kc n", p=128), "s_cast", ("s_ff1", g)))
            for g in range(2):
                casts.append((s_gate[g], w_gate[:, g * 512:(g + 1) * 512].rearrange(
                    "(kc p) n -> p kc n", p=128), "s_cast", ("s_gate", g)))
            for c in range(8):
                casts.append((s_ff2[c], w_ff2[:, c * 128:(c + 1) * 128].rearrange(
                    "(fc p) n -> p fc n", p=128), "s_cast", ("s_ff2", c)))

            def issue_casts(cnt):
                for _ in range(cnt):
                    if casts:
                        dst, src, key, res = casts.pop(0)
                        h.dma("pool", dst, src, key, [], [res])

            U = [ubuf[j // 2][:, (j % 2) * 514:(j % 2 + 1) * 514] for j in range(4)]
            rot_gen = setup_rotary()
            it = 0
            for tt in range(4):
                for j in range(4):
                    ub = U[j]
                    r_u = ("U", j)
                    if tt == 0:
                        h.P.op("dve", lambda e, ub=ub: e.memset(ub[:, 0:2], 0.0), [], [r_u])
                    par = it % 2
                    it += 1
                    tok = slice(tt * 512, (tt + 1) * 512)
                    r_xT = ("xT", tt)
                    banks = (pf[par], pf[2 + par], pf[4 + par])
                    rb = (PF[par], PF[2 + par], PF[4 + par])
                    for gi, slab in enumerate((wB, wC, wH)):
                        for kc in range(8):
                            h.mm(banks[gi][:], slab[:, kc, j * 128:(j + 1) * 128], xT[:, kc, tok],
                                 kc == 0, kc == 7, [("wslj", gi, j), r_xT], [rb[gi]])
                    hsb = hs[par]
                    ab = abuf[par]
                    h.cp("act", hsb[:], banks[2][:], [rb[2]], [("hs", par)])
                    u_t = ub[:, 2:514]
                    u_m1 = ub[:, 1:513]
                    u_m2 = ub[:, 0:512]
                    h.tt("dve", u_t, banks[1][:], hsb[:], ALU.mult, [rb[1], ("hs", par)], [r_u])
                    h.act(ab[:], u_t, AF.Copy, [r_u, "smallT"], [("ab", par)],
                          scale=cwT[:, 8 + j:9 + j])
                    h.stt(ab[:], u_m1, cwT[:, 4 + j:5 + j], ab[:], ALU.mult, ALU.add,
                          [r_u, ("ab", par), "smallT"], [("ab", par)])
                    h.stt(ab[:], u_m2, cwT[:, j:j + 1], ab[:], ALU.mult, ALU.add,
                          [r_u, ("ab", par), "smallT"], [("ab", par)])
                    h.tt("dve", ycat[:, j, tok], banks[0][:], ab[:], ALU.mult,
                         [rb[0], ("ab", par)], [("ycat", tt)])
                    if tt < 3:
                        h.cp("dve", ub[:, 0:2], ub[:, 512:514], [r_u], [r_u])
                    if tt >= 1:
                        for _ in range(2):
                            next(rot_gen, None)
                        if j % 2 == 1:
                            issue_casts(1)
            for _ in rot_gen:
                pass

            q_ps, k_ps, v_ps, s_ps, o_ps, kv_ps = pf
            rq, rk, rv, rs, ro, rkv = PF

            def ctok_(n):
                return slice(n * 128, (n + 1) * 128)

            def rx(n):
                return ("xT", n // 4)

            def P1(n):
                for (bank, rbk, slab, si) in ((v_ps, rv, wV, 5), (q_ps, rq, wQ, 3), (k_ps, rk, wK, 4)):
                    for kc in range(8):
                        h.mm(bank[:], xT[:, kc, ctok_(n)], slab[:, kc, :], kc == 0, kc == 7,
                             [("wsl", si), rx(n)], [rbk])

            def E1(n):
                cosb = cosF[:, n, :].unsqueeze(1).broadcast_to([128, 4, 128])
                sin_lo = sinS[:, n, 0:64].unsqueeze(1).broadcast_to([128, 4, 64])
                sin_hi = sinS[:, n, 64:128].unsqueeze(1).broadcast_to([128, 4, 64])
                h.cp("act", vb[n % 4][:], v_ps[:], [rv], [("vb", n % 4)])
                h.tt("dve", f4(vz[n % 4][:]), f4(v_ps[:]),
                     zeta[:].unsqueeze(2).broadcast_to([128, 4, 128]), ALU.mult,
                     [rv, "zeta"], [("vz", n % 4)])
                for ti, (bank, rbk, dst, dname) in enumerate(
                        ((q_ps, rq, qr[n % 2], ("qr", n % 2)), (k_ps, rk, kr[n % 4], ("kr", n % 4)))):
                    b4 = f4(bank[:])
                    a1, a2 = t1[ti], t2[ti]
                    h.tt("dve", a1, b4, cosb, ALU.mult, [rbk, "cosF"], [("t1", ti)])
                    h.tt("dve", a2[:, :, 0:64], b4[:, :, 64:128], sin_lo, ALU.mult,
                         [rbk, "sinS"], [("t2", ti)])
                    h.tt("dve", a2[:, :, 64:128], b4[:, :, 0:64], sin_hi, ALU.mult,
                         [rbk, "sinS"], [("t2", ti)])
                    h.tt("pool", f4(dst[:]), a1, a2, ALU.add, [("t1", ti), ("t2", ti)], [dname])

            def P2(n):
                for hh in range(4):
                    h.tr(pb[0][:, hh, :], qr[n % 2][:, hh * 128:(hh + 1) * 128], ident[:],
                         [("qr", n % 2), "ident"], [PB[0]])
                for hh in range(4):
                    h.tr(pb[0][:, 4 + hh, :], kr[n % 4][:, hh * 128:(hh + 1) * 128], ident[:],
                         [("kr", n % 4), "ident"], [PB[0]])

            def E2(n):
                h.tt("dve", qhT[n % 3][:], pb[0][:, 0:4, :], xiT[:], ALU.mult, [PB[0], "xiT"],
                     [("qhT", n % 3)])
                h.cp("act", kT[n % 2][:], pb[0][:, 4:8, :], [PB[0]], [("kT", n % 2)])

            def P3(n):
                for hh in range(4):
                    h.mm(s_ps[:, hh * 128:(hh + 1) * 128], kT[n % 2][:, hh, :], qhT[n % 3][:, hh, :],
                         True, True, [("kT", n % 2), ("qhT", n % 3)], [rs])

            def E3(n):
                h.tt("dve", PT[n % 2][:], f4(s_ps[:]), maskT[:], ALU.mult, [rs, "maskT"],
                     [("PT", n % 2)])

            def P4(n):
                for hh in range(4):
                    hs_ = slice(hh * 128, (hh + 1) * 128)
                    h.mm(o_ps[:, hs_], PT[n % 2][:, hh, :], vb[n % 4][:, hs_], True, n == 0,
                         [("PT", n % 2), ("vb", n % 4)], [ro])
                    if n > 0:
                        h.mm(o_ps[:, hs_], qhT[n % 3][:, hh, :], Sb[n % 2][:, hs_], False, True,
                             [("qhT", n % 3), ("Sb", n % 2)], [ro])
                if n < NB - 1:
                    for hh in range(4):
                        hs_ = slice(hh * 128, (hh + 1) * 128)
                        h.mm(kv_ps[:, hs_], kr[n % 4][:, hs_], vz[n % 4][:, hs_], True, True,
                             [("kr", n % 4), ("vz", n % 4)], [rkv])

            def E4a(n):
                if n < NB - 1:
                    if n == 0:
                        h.cp("dve", Sst[:], kv_ps[:], [rkv], ["Sst"])
                    else:
                        h.tt("dve", Sst[:], kv_ps[:], Stmp[:], ALU.add, [rkv, "Stmp"], ["Sst"])
                    h.cp("act", Sb[(n + 1) % 2][:], Sst[:], ["Sst"], [("Sb", (n + 1) % 2)])
                    if n + 1 < NB - 1:
                        h.tt("pool", Stmp[:], Sst[:], cdT[:], ALU.mult, ["Sst", "cdT"], ["Stmp"])

            def P4g(n):
                for hh in range(4):
                    for kc in range(8):
                        h.mm(v_ps[:, hh * 128:(hh + 1) * 128], wG[:, kc, hh * 128:(hh + 1) * 128],
                             xT[:, kc, ctok_(n)], kc == 0, kc == 7, [("wsl", 6), rx(n)], [rv])
                h.act(gsr[n % 2], f4(v_ps[:]), AF.Silu, [rv], [("gsr", n % 2)])

            def E4b(n):
                i = n % 2
                sum4 = st4[:, i, 0:4]
                ssq4 = st4[:, i, 4:8]
                mean4 = st4[:, i, 8:12]
                m24 = st4[:, i, 12:16]
                var4 = st4[:, i, 16:20]
                rstd4 = st4[:, i, 20:24]
                nb4 = st4[:, i, 24:28]
                rs4 = ("st4", i)
                o4 = f4(o_ps[:])
                rsa = ("st4a", i)
                rsb = ("st4b", i)
                h.red(sum4, o4, [ro], [rsa])
                h.act(sq, o4, AF.Square, [ro], ["sq"])
                h.red(ssq4, sq, ["sq"], [rsb])
                h.ts("pool", mean4, sum4, 1.0 / 128.0, ALU.mult, [rsa], [rs4], s2=0.0, op1=ALU.add)
                h.tt("pool", m24, mean4, mean4, ALU.mult, [rs4], [rs4])
                h.ts("pool", var4, ssq4, 1.0 / 128.0, ALU.mult, [rsb, rs4], [rs4], s2=GN_EPS,
                     op1=ALU.add)
                h.tt("pool", var4, var4, m24, ALU.subtract, [rs4], [rs4])
                h.tt("pool", rstd4, var4, neghalf[:], ALU.pow, [rs4, "neghalf"], [rs4])
                h.tt("pool", nb4, mean4, rstd4, ALU.mult, [rs4], [rs4])
                h.ts("pool", nb4, nb4, -1.0, ALU.mult, [rs4], [rs4], s2=0.0, op1=ALU.add)
                for hh in range(4):
                    h.act(ob[i][:, hh * 128:(hh + 1) * 128], o_ps[:, hh * 128:(hh + 1) * 128],
                          AF.Identity, [ro, rs4], [("ob", i)],
                          scale=st4[:, i, 20 + hh:21 + hh], bias=st4[:, i, 24 + hh:25 + hh])

            def P5(n):
                for hh in range(4):
                    h.tr(pb[1][:, hh, :], ob[n % 2][:, hh * 128:(hh + 1) * 128], ident[:],
                         [("ob", n % 2), "ident"], [PB[1]])

            def E5(n):
                i = n % 2
                for hh in range(4):
                    h.act(yn[i][:, hh, :], pb[1][:, hh, :], AF.Identity, [PB[1], "smallT"],
                          [("yn", i)], scale=gnwT[:, hh:hh + 1], bias=gnbT[:, hh:hh + 1])
                h.tt("pool", ycat[:, 4:8, ctok_(n)], yn[i], gsr[i], ALU.mult,
                     [("yn", i), ("gsr", i)], [("ycat", n // 4)])

            def ok(n):
                return 0 <= n < NB

            gen_lnA = make_gen_ln(P, h, lnstA, neghalf)
            ln1wA = lnA[:, 0:1024]
            ln1bA = lnA[:, 1024:2048]

            def b0_setup():
                h.dma("sp", wsl[3][:], s_wout[0], ("b0k", 0), [("s_wout", 0)], [("wsl", 3)])
                h.dma("sp", wsl[4][:], s_wout[1], ("b0k", 1), [("s_wout", 1)], [("wsl", 4)])
                h.dma("sp", ln1wA, ln1_w.partition_broadcast(128), ("b0l", 0), [], ["ln1wA"])
                h.dma("sp", ln1bA, ln1_b.partition_broadcast(128), ("b0l", 1), [], ["ln1bA"])

            def b0_mix(bi):
                h.dma("sp", xinA[bi % 2], x[bi * 128:(bi + 1) * 128, :], ("xinA", bi % 2),
                      [], [("xinA", bi % 2)])
                bank, rbk = pf[bi % 2], PF[bi % 2]
                for hf in range(2):
                    for kc in range(8):
                        h.mm(bank[:], ycat[:, kc, bi * 128:(bi + 1) * 128], wsl[3 + hf][:, kc, :],
                             kc == 0, kc == 7, [("ycat", 0), ("wsl", 3 + hf)], [rbk])
                    h.stt(x1v(bi)[:, hf * 512:(hf + 1) * 512], xinA[bi % 2][:, hf * 512:(hf + 1) * 512],
                          ALPHA, bank[:], ALU.mult, ALU.add, [("xinA", bi % 2), rbk], [("x1", bi)])

            def gen_b0norm(bi):
                yield from gen_lnA(x1v(bi), ("x1", bi), ln1wA, ln1bA, "ln1wA", "ln1bA", bi)
                h.cp("act", x1bA[bi], x1v(bi), [("x1", bi)], [("x1bA", bi)])
                yield
                yield
                yield
                for kc in range(8):
                    h.tr(pb[0][:, kc, :], x1bA[bi][:, kc * 128:(kc + 1) * 128], ident[:],
                         [("x1bA", bi), "ident"], [PB[0]])
                h.cp("act", ycat[:, :, bi * 128:(bi + 1) * 128], pb[0][:], [PB[0]], [("ycat", 0)])
                yield

            b0_gens = []

            def b0_advance(k):
                for _ in range(k):
                    for g in list(b0_gens):
                        try:
                            next(g)
                        except StopIteration:
                            b0_gens.remove(g)

            P1(0)
            for t in range(NB + 4):
                drain = t >= 15
                if ok(t):
                    E1(t)
                if drain:
                    b0_advance(1)
                if ok(t - 1):
                    P2(t - 1)
                    E2(t - 1)
                if drain:
                    b0_advance(1)
                if ok(t - 2):
                    P3(t - 2)
                    E3(t - 2)
                if drain:
                    b0_advance(1)
                if ok(t - 3):
                    P4g(t - 3)
                    P4(t - 3)
                    E4a(t - 3)
                    E4b(t - 3)
                if drain:
                    b0_advance(2)
                if ok(t - 4):
                    P5(t - 4)
                    E5(t - 4)
                if ok(t + 1):
                    P1(t + 1)
                issue_casts(1)
                if t == 15:
                    b0_setup()
                if t == 17:
                    h.dma("sp", kpre[:], s_ff1[0], "kpre", [("s_ff1", 0)], ["kpre"])
                if t in (15, 16):
                    for bi in (2 * (t - 15), 2 * (t - 15) + 1):
                        b0_mix(bi)
                        b0_gens.append(gen_b0norm(bi))
                if drain:
                    b0_advance(3)
            issue_casts(len(casts))
            while b0_gens:
                b0_advance(1)

            P.join_all()
            P.emit(st, "a", st0)

        with ExitStack() as st:
            def sb(name, shape, dt):
                return st.enter_context(nc.sbuf_tensor(name, shape, dt))

            def psf(name):
                return st.enter_context(nc.psum_tensor(name, [128, 512], F32))

            def psb(name):
                return st.enter_context(nc.psum_tensor(name, [128, 8, 128], BF16))

            P = Prog(nc)
            h = H(P)

            lnt = [sb("lnt%d" % i, [128, D], F32) for i in range(4)]
            ln1w_t, ln1b_t, ln2w_t, ln2b_t = lnt
            wproj = sb("wproj", [128, 2, D], BF16)
            acc2 = sb("acc2", [128, 4, D], F32)
            hT = sb("hT", [128, 32, 512], BF16)
            NK = 3
            NC2 = 3
            kslab = [sb("kslab%d" % i, [128, 8, 512], BF16) for i in range(NK)]
            cslab = [sb("cslab%d" % i, [128, 32, 128], BF16) for i in range(NC2)]
            xin = [sb("xin%d" % i, [128, D], F32) for i in range(2)]
            x1b = [sb("x1b%d" % i, [128, D], BF16) for i in range(4)]
            pbf = [sb("pbf%d" % i, [128, 256], BF16) for i in range(4)]
            pT = sb("pT", [128, 4, 2, 128], BF16)
            sg = [sb("sg%d" % i, [128, 512], F32) for i in range(2)]
            tR = [sb("tR%d" % i, [128, 512], F32) for i in range(2)]
            ffs = [sb("ffs%d" % i, [128, 512], F32) for i in range(2)]
            lnst = sb("lnst", [128, 4, 16], F32)

            pm = [psf("pm%d" % i) for i in range(4)]
            pF = [psf("pF%d" % i) for i in range(2)]
            pT16 = psb("pT16")
            pT32 = psf("pT32")
            RM = [("ps", i) for i in range(4)]
            RF = [("ps", 4 + i) for i in range(2)]
            RT16 = ("ps", 6)
            RT32 = ("ps", 7)


            kcnt = [0]
            ccnt = [0]

            def load_k(src):
                i = kcnt[0] % NK
                kcnt[0] += 1
                h.dma("sp", kslab[i][:], src, ("kslab", i), [], [("kslab", i)])
                return kslab[i], ("kslab", i)

            def load_c(src):
                i = ccnt[0] % NC2
                ccnt[0] += 1
                h.dma("sp", cslab[i][:], src, ("cslab", i), [], [("cslab", i)])
                return cslab[i], ("cslab", i)

            fcnt = [0]

            def next_F():
                i = fcnt[0] % 2
                fcnt[0] += 1
                return pF[i], RF[i], i

            gen_ln = make_gen_ln(P, h, lnst, neghalf)

            def drain(g):
                for _ in g:
                    pass

            def drain_rr(gens):
                gens = list(gens)
                while gens:
                    for g in list(gens):
                        try:
                            next(g)
                        except StopIteration:
                            gens.remove(g)

            bgq = []

            def bg_step(k=1):
                for _ in range(k):
                    while bgq:
                        try:
                            next(bgq[0])
                            break
                        except StopIteration:
                            bgq.pop(0)

            def bg_flush():
                while bgq:
                    drain(bgq.pop(0))

            def B_mix_block(tt, bi, slabs):
                blk = tt * 4 + bi
                h.dma("sp", xin[bi % 2][:], x[blk * 128:(blk + 1) * 128, :], ("xin", bi % 2),
                      [], [("xin", bi % 2)])
                for hf in range(2):
                    slab, rsl = slabs[hf]
                    bk = (2 * bi + hf) % 4
                    for kc in range(8):
                        h.mm(pm[bk][:], ycat[:, kc, blk * 128:(blk + 1) * 128], slab[:, kc, :],
                             kc == 0, kc == 7, [("ycat", tt), rsl], [RM[bk]])
                    h.stt(x1v(bi)[:, hf * 512:(hf + 1) * 512], xin[bi % 2][:, hf * 512:(hf + 1) * 512],
                          ALPHA, pm[bk][:], ALU.mult, ALU.add, [("xin", bi % 2), RM[bk]],
                          [("x1", bi)])

            def phase_B_mix(tt):
                slabs = [load_k(s_wout[0]), load_k(s_wout[1])]
                for bi in range(4):
                    B_mix_block(tt, bi, slabs)

            def gen_bnorm(tt, bi):
                yield from gen_ln(x1v(bi), ("x1", bi), ln1w_t[:], ln1b_t[:], "ln1w", "ln1b", bi)
                h.cp("act", x1b[bi][:], x1v(bi), [("x1", bi)], [("x1b", bi)])
                yield
                yield
                yield
                for kc in range(8):
                    h.tr(pT16[:, kc, :], x1b[bi][:, kc * 128:(kc + 1) * 128], ident[:],
                         [("x1b", bi), "ident"], [RT16])
                h.cp("act", ycat[:, :, (tt * 4 + bi) * 128:(tt * 4 + bi + 1) * 128], pT16[:],
                     [RT16], [("ycat", tt)])
                yield

            def phase_B_norm(tt):
                drain_rr(gen_bnorm(tt, bi) for bi in range(4))

            def phase_ff1(tt):
                tok = slice(tt * 512, (tt + 1) * 512)
                for g in range(8):
                    if tt == 0 and g == 0:
                        slab, rsl = kpre, "kpre"
                    else:
                        slab, rsl = load_k(s_ff1[g])
                    for f4 in range(4):
                        fc = g * 4 + f4
                        bank, rbk, i = next_F()
                        for kc in range(8):
                            h.mm(bank[:], slab[:, kc, f4 * 128:(f4 + 1) * 128], ycat[:, kc, tok],
                                 kc == 0, kc == 7, [rsl, ("ycat", tt)], [rbk])
                        h.act(tR[i][:], bank[:], AF.Relu, [rbk], [("tR", i)])
                        h.tt("dve", hT[:, fc, :], bank[:], tR[i][:], ALU.mult, [rbk, ("tR", i)],
                             [("hT", fc)])
                        bg_step(2)

            def load_p(tt, bi):
                blk = tt * 4 + bi
                h.dma("pool", pbf[bi][:], p_in[blk * 128:(blk + 1) * 128, :], ("pbf", bi),
                      [], [("pbf", bi)])

            def phase_gate_ple(tt):
                for bi in range(4):
                    for k2 in range(2):
                        h.tr(pT16[:, k2, :], pbf[bi][:, k2 * 128:(k2 + 1) * 128], ident[:],
                             [("pbf", bi), "ident"], [RT16])
                    h.cp("act", pT[:, bi, :, :], pT16[:, 0:2, :], [RT16], [("pT", bi)])
                slabs = [load_k(s_gate[0]), load_k(s_gate[1])]
                cnt = 0
                for hf in range(2):
                    slab, rsl = slabs[hf]
                    cols = slice(hf * 512, (hf + 1) * 512)
                    for bi in range(4):
                        for kc in range(8):
                            h.mm(pm[bi][:], ycat[:, kc, (tt * 4 + bi) * 128:(tt * 4 + bi + 1) * 128],
                                 slab[:, kc, :], kc == 0, kc == 7, [("ycat", tt), rsl], [RM[bi]])
                        i = cnt % 2
                        cnt += 1
                        h.act(sg[i][:], pm[bi][:], AF.Sigmoid, [RM[bi]], [("sg", i)])
                        bank, rbk, _ = next_F()
                        for k2 in range(2):
                            h.mm(bank[:], pT[:, bi, k2, :], wproj[:, k2, cols], k2 == 0, k2 == 1,
                                 [("pT", bi), "wproj"], [rbk])
                        h.tt("dve", acc2[:, bi, cols], bank[:], sg[i][:], ALU.mult,
                             [rbk, ("sg", i)], [("acc2", bi)])
                        h.stt(acc2[:, bi, cols], x1v(bi)[:, cols], ALPHA, acc2[:, bi, cols],
                              ALU.mult, ALU.add, [("x1", bi), ("acc2", bi)], [("acc2", bi)])

            def phase_ff2(tt):
                a4 = acc2[:]
                pend = None

                def fin(c, i):
                    for bi in range(4):
                        h.tr(pT32[:, bi * 128:(bi + 1) * 128], ffs[i][:, bi * 128:(bi + 1) * 128],
                             identf[:], [("ffs", i), "identf"], [RT32])
                    h.tt("dve", a4[:, :, c * 128:(c + 1) * 128],
                         pT32[:].rearrange("p (b d) -> p b d", b=4), a4[:, :, c * 128:(c + 1) * 128],
                         ALU.add, [RT32] + [("acc2", b) for b in range(4)],
                         [("acc2", b) for b in range(4)])

                for c in range(8):
                    slab, rsl = load_c(s_ff2[c])
                    bank, rbk, i = next_F()
                    for fc in range(32):
                        h.mm(bank[:], slab[:, fc, :], hT[:, fc, :], fc == 0, fc == 31,
                             [rsl, ("hT", fc)], [rbk])
                        if fc % 8 == 7:
                            bg_step(3)
                        if fc == 7 and pend is not None:
                            fin(*pend)
                            pend = None
                    h.cp("act", ffs[i][:], bank[:], [rbk], [("ffs", i)])
                    pend = (c, i)
                fin(*pend)

            def gen_out(tt, bi):
                blk = tt * 4 + bi
                yield from gen_ln(acc2[:, bi, :], ("acc2", bi), ln2w_t[:], ln2b_t[:], "ln2w", "ln2b", bi)
                h.dma("pool", out[blk * 128:(blk + 1) * 128, :], acc2[:, bi, :], ("outq", bi),
                      [("acc2", bi)], [("out", blk)])
                yield

            h.dma("act", ln1w_t[:], ln1_w.partition_broadcast(128), "lnt0", [], ["ln1w"])
            h.dma("act", ln1b_t[:], ln1_b.partition_broadcast(128), "lnt1", [], ["ln1b"])
            h.dma("pool", wproj[:], w_proj.rearrange("(kc p) n -> p kc n", p=128), "wproj", [],
                  ["wproj"])
            h.dma("act", ln2w_t[:], ln2_w.partition_broadcast(128), "lnt2", [], ["ln2w"])
            h.dma("act", ln2b_t[:], ln2_b.partition_broadcast(128), "lnt3", [], ["ln2b"])
            for tt in range(4):
                if tt > 0:
                    for bi in range(4):
                        bgq.append(gen_out(tt - 1, bi))
                for bi in range(4):
                    load_p(tt, bi)
                phase_ff1(tt)
                bg_flush()
                phase_gate_ple(tt)
                if tt < 3:
                    phase_B_mix(tt + 1)
                    for bi in range(4):
                        bgq.append(gen_bnorm(tt + 1, bi))
                phase_ff2(tt)
                bg_flush()
            drain_rr(gen_out(3, bi) for bi in range(4))
            P.join_all()
            P.emit(st, "b", st0)
    return nc


def _consts():
    hh = np.arange(4, dtype=np.float64)
    log_g = np.log1p(-np.exp2(-5.0 - hh))
    idx = np.arange(128, dtype=np.float64)
    invf = (10000.0 ** (-np.arange(64, dtype=np.float32) * 2.0 / 128)).astype(np.float32)
    xi = np.exp((idx + 1.0)[None, :] * log_g[:, None])
    mj = np.exp(-(idx + 1.0)[:, None] * log_g[None, :]) * (128.0 ** -0.5)
    causal = (idx[None, :] >= idx[:, None]).astype(np.float64)
    mask = mj[:, :, None] * causal[:, None, :]
    zeta = np.exp((127.0 - idx)[:, None] * log_g[None, :]) * (128.0 ** -0.5)
    cd = np.repeat(np.exp(128.0 * log_g), 128)
    return dict(
        c_invf=invf,
        c_xi=xi.reshape(512).astype(np.float32),
        c_mask=mask.reshape(128, 512).astype(np.float32),
        c_zeta=zeta.astype(np.float32),
        c_cd=cd.astype(np.float32),
    )


_NC_CACHE = {}


def kernel(x, p, positions, w_in, conv_w, ret_gn_w, ret_gn_b, w_out, ln1_w, ln1_b,
           w_ff1, w_ff2, w_ple_gate, w_ple_proj, ln2_w, ln2_b):
    n = 8
    f = lambda a: np.ascontiguousarray(np.asarray(a))
    x = f(x); p = f(p); positions = f(positions)
    shared = dict(
        w_in=f(w_in)[0], conv_w=f(conv_w)[0], gn_w=f(ret_gn_w)[0], gn_b=f(ret_gn_b)[0],
        w_out=f(w_out)[0], ln1_w=f(ln1_w)[0], ln1_b=f(ln1_b)[0], w_ff1=f(w_ff1)[0],
        w_ff2=f(w_ff2)[0], w_gate=f(w_ple_gate)[0], w_proj=f(w_ple_proj)[0],
        ln2_w=f(ln2_w)[0], ln2_b=f(ln2_b)[0],
    )
    shared.update(_consts())
    if "nc" not in _NC_CACHE:
        _NC_CACHE["nc"] = build_nc()
    nc = _NC_CACHE["nc"]
    in_maps = []
    for b in range(n):
        m = dict(shared)
        m["x"] = f(x[b])
        m["p"] = f(p[0, b])
        m["pos"] = f(positions[b]).astype(np.int32)
        in_maps.append(m)
    res = run_bass_kernel_spmd(nc, in_maps, core_ids=list(range(n)))
    return np.stack([np.asarray(r["out"]) for r in res.results], axis=0).astype(np.float32)
```

```python
from contextlib import ExitStack
import numpy as np
import concourse.bass as bass
import concourse.mybir as mybir
from concourse.bass_utils import run_bass_kernel_spmd

F32 = mybir.dt.float32
BF16 = mybir.dt.bfloat16
I32 = mybir.dt.int32
AF = mybir.ActivationFunctionType
ALU = mybir.AluOpType
AX = mybir.AxisListType

ENGS = ("pe", "act", "dve", "pool", "sp")
STRICT_SYNC = True

S = 2048
D = 1024
NB = 16
ALPHA = float(2.0 ** 0.25)
LN_EPS = 1e-5
GN_EPS = 1e-5
PI = float(np.pi)


class _Op:
    __slots__ = ("eng", "fn", "deps", "is_dma", "sem_key", "signals", "sig_val", "idx")


def _excl(r):
    return isinstance(r, tuple) and r[0] == "ps"


class Prog:
    def __init__(self, nc):
        self.nc = nc
        self.ops = []
        self.last_writer = {}
        self.readers = {}
        self.written = set()
        self.alias = {}
        self._alias_seen = set()

    def _add(self, eng, fn, reads, writes, is_dma=False, sem_key=None):
        extra = []
        for r in list(reads) + list(writes):
            if r in self.alias and r not in self._alias_seen:
                self._alias_seen.add(r)
                pa = self.alias[r]
                extra.extend(pa if isinstance(pa, list) else [pa])
        if extra:
            writes = list(writes) + extra
        o = _Op()
        o.eng, o.fn, o.is_dma, o.sem_key = eng, fn, is_dma, sem_key
        o.signals = is_dma
        o.sig_val = 0
        o.idx = len(self.ops)
        deps = {}
        for r in reads:
            w = self.last_writer.get(r)
            if w is not None:
                deps[w] = True
            if _excl(r):
                rd = self.readers.get(r)
                if rd:
                    for x in rd.values():
                        deps.setdefault(x, False)
        for r in writes:
            w = self.last_writer.get(r)
            if w is not None:
                deps.setdefault(w, False)
            rd = self.readers.get(r)
            if rd:
                for x in rd.values():
                    deps.setdefault(x, False)
        o.deps = deps
        for r in reads:
            d = self.readers.setdefault(r, {})
            key = ("dma", o.idx) if is_dma else eng
            d[key] = o.idx
        for r in writes:
            self.last_writer[r] = o.idx
            self.readers[r] = {}
            self.written.add(r)
        self.ops.append(o)
        return o

    def op(self, eng, fn, reads=(), writes=()):
        return self._add(eng, fn, reads, writes)

    def dma(self, eng, fn, sem_key, reads=(), writes=()):
        return self._add(eng, fn, reads, writes, is_dma=True, sem_key=sem_key)

    def wait_all(self, eng, resources):
        return self._add(eng, None, list(resources), [])

    def join_all(self):
        res = list(self.written)
        last = {}
        for o in self.ops:
            if o.fn is not None:
                last[o.eng] = o.idx
        for e in ENGS:
            o = self._add(e, None, res, [])
            for le, li in last.items():
                if le != e:
                    o.deps.setdefault(li, True)

    def emit(self, stack, tag, sem_stack=None):
        nc = self.nc
        sem_stack = sem_stack or stack
        ops = self.ops
        for o in ops:
            nd = set()
            for d, raw in o.deps.items():
                p = ops[d]
                if p.fn is None or d == o.idx:
                    continue
                if (not p.is_dma) and p.eng == o.eng:
                    if o.eng == "pe" or not (raw or STRICT_SYNC):
                        continue
                nd.add(d)
            o.deps = nd
            for d in nd:
                ops[d].signals = True
        esem = {e: sem_stack.enter_context(nc.semaphore("sem_%s_%s" % (tag, e))) for e in ENGS}
        dsem = {}
        ecount = {e: 0 for e in ENGS}
        dcount = {}
        for o in ops:
            if o.is_dma:
                if o.sem_key not in dsem:
                    dsem[o.sem_key] = sem_stack.enter_context(
                        nc.semaphore("dsem_%s_%d" % (tag, len(dsem))))
                    dcount[o.sem_key] = 0
                dcount[o.sem_key] += 16
                o.sig_val = dcount[o.sem_key]
            elif o.signals and o.fn is not None:
                ecount[o.eng] += 1
                o.sig_val = ecount[o.eng]

        def sem_of(p):
            return dsem[p.sem_key] if p.is_dma else esem[p.eng]

        block = stack.enter_context(nc.Block())

        def run_engine(ename):
            def body(e):
                waited = {}
                for o in ops:
                    if o.eng != ename:
                        continue
                    need = {}
                    for d in o.deps:
                        p = ops[d]
                        s = sem_of(p)
                        k = id(s)
                        if k not in need or need[k][1] < p.sig_val:
                            need[k] = (s, p.sig_val)
                    for k, (s, v) in need.items():
                        if waited.get(k, 0) >= v:
                            continue
                        e.wait_ge(s, v)
                        waited[k] = v
                    if o.fn is None:
                        continue
                    inst = o.fn(e)
                    if o.is_dma:
                        inst.then_inc(dsem[o.sem_key], 16)
                    elif o.signals:
                        inst.then_inc(esem[ename], 1)
            return body

        block.tensor(run_engine("pe"))
        block.scalar(run_engine("act"))
        block.vector(run_engine("dve"))
        block.gpsimd(run_engine("pool"))
        block.sync(run_engine("sp"))


class H:
    def __init__(self, P):
        self.P = P

    def mm(self, out, lhsT, rhs, start, stop, reads, writes):
        self.P.op("pe", lambda e: e.matmul(out, lhsT=lhsT, rhs=rhs, start=start, stop=stop),
                  reads, writes)

    def tr(self, out, in_, ident, reads, writes):
        self.P.op("pe", lambda e: e.transpose(out=out, in_=in_, identity=ident), reads, writes)

    def tt(self, eng, out, in0, in1, op, reads, writes):
        self.P.op(eng, lambda e: e.tensor_tensor(out=out, in0=in0, in1=in1, op=op), reads, writes)

    def ts(self, eng, out, in0, s1, op0, reads, writes, s2=None, op1=None):
        if op1 is None:
            self.P.op(eng, lambda e: e.tensor_scalar(out=out, in0=in0, scalar1=s1, scalar2=None,
                                                     op0=op0), reads, writes)
        else:
            self.P.op(eng, lambda e: e.tensor_scalar(out=out, in0=in0, scalar1=s1, scalar2=s2,
                                                     op0=op0, op1=op1), reads, writes)

    def stt(self, out, in0, scalar, in1, op0, op1, reads, writes):
        self.P.op("dve", lambda e: e.scalar_tensor_tensor(out=out, in0=in0, scalar=scalar, in1=in1,
                                                          op0=op0, op1=op1), reads, writes)

    def act(self, out, in_, func, reads, writes, scale=None, bias=None):
        kw = {}
        if scale is not None:
            kw["scale"] = scale
        if bias is not None:
            kw["bias"] = bias
        self.P.op("act", lambda e: e.activation(out=out, in_=in_, func=func, **kw), reads, writes)

    def cp(self, eng, out, in_, reads, writes):
        if eng == "act":
            self.act(out, in_, AF.Copy, reads, writes)
        else:
            self.P.op(eng, lambda e: e.tensor_copy(out=out, in_=in_), reads, writes)

    def red(self, out, in_, reads, writes):
        self.P.op("dve", lambda e: e.tensor_reduce(out=out, in_=in_, op=ALU.add, axis=AX.X),
                  reads, writes)

    def dma(self, eng, out, in_, key, reads, writes, **kw):
        self.P.dma(eng, lambda e: e.dma_start(out=out, in_=in_, **kw), key, reads, writes)


def make_gen_ln(P, h, lnst, neghalf):
    def gen_ln(buf, rbuf, w_ap, b_ap, rw, rb, slot, bias_eng="dve"):
        st6 = lnst[:, slot, 0:12]
        mv = lnst[:, slot, 12:14]
        rstd = lnst[:, slot, 14:15]
        nbias = lnst[:, slot, 15:16]
        rl = ("lnst", slot)
        for hf in range(2):
            P.op("dve", lambda e, hf=hf: e.bn_stats(out=st6[:, hf * 6:(hf + 1) * 6],
                                                    in_=buf[:, hf * 512:(hf + 1) * 512]),
                 [rbuf], [rl])
            yield
        P.op("dve", lambda e: e.bn_aggr(out=mv, in_=st6), [rl], [rl])
        yield
        h.ts("dve", rstd, mv[:, 1:2], LN_EPS, ALU.add, [rl], [rl])
        yield
        h.tt("pool", rstd, rstd, neghalf[:, 0:1], ALU.pow, [rl, "neghalf"], [rl])
        yield
        yield
        h.stt(nbias, mv[:, 0:1], -1.0, rstd, ALU.mult, ALU.mult, [rl], [rl])
        yield
        h.act(buf, buf, AF.Identity, [rbuf, rl], [rbuf], scale=rstd, bias=nbias)
        yield
        yield
        h.tt("dve", buf, buf, w_ap, ALU.mult, [rbuf, rw], [rbuf])
        yield
        h.tt(bias_eng, buf, buf, b_ap, ALU.add, [rbuf, rb], [rbuf])
        yield
    return gen_ln


def build_nc():
    nc = bass.Bass("TRN2", target_bir_lowering=False)

    def din(name, shape, dt=F32):
        return nc.dram_tensor(name, shape, dt, kind="ExternalInput").ap()

    x = din("x", [S, D])
    p_in = din("p", [S, 256])
    pos = din("pos", [S], I32)
    w_in = din("w_in", [D, 3584])
    conv_w = din("conv_w", [3, 512])
    gn_w = din("gn_w", [512])
    gn_b = din("gn_b", [512])
    w_out = din("w_out", [D, D])
    ln1_w = din("ln1_w", [D])
    ln1_b = din("ln1_b", [D])
    w_ff1 = din("w_ff1", [D, 4096])
    w_ff2 = din("w_ff2", [4096, D])
    w_gate = din("w_gate", [D, D])
    w_proj = din("w_proj", [256, D])
    ln2_w = din("ln2_w", [D])
    ln2_b = din("ln2_b", [D])
    c_invf = din("c_invf", [64])
    c_xi = din("c_xi", [512])
    c_mask = din("c_mask", [128, 512])
    c_zeta = din("c_zeta", [128, 4])
    c_cd = din("c_cd", [512])
    out = nc.dram_tensor("out", [S, D], F32, kind="ExternalOutput").ap()

    s_ff1 = nc.dram_tensor("s_ff1", [8, 128, 8, 512], BF16).ap()
    s_ff2 = nc.dram_tensor("s_ff2", [8, 128, 32, 128], BF16).ap()
    s_wout = nc.dram_tensor("s_wout", [2, 128, 8, 512], BF16).ap()
    s_gate = nc.dram_tensor("s_gate", [2, 128, 8, 512], BF16).ap()

    with ExitStack() as st0:
        def sb0(name, shape, dt):
            return st0.enter_context(nc.sbuf_tensor(name, shape, dt))

        ycat = sb0("ycat", [128, 8, S], BF16)
        ident = sb0("ident", [128, 128], BF16)
        identf = sb0("identf", [128, 128], F32)
        neghalf = sb0("neghalf", [128, 4], F32)
        ubuf = [sb0("ubuf%d" % i, [128, 2 + S], F32) for i in range(2)]

        def x1v(bi):
            return ubuf[bi // 2][:, (bi % 2) * 1024:(bi % 2 + 1) * 1024]
        kpre = sb0("kpre", [128, 8, 512], BF16)

        with ExitStack() as st:
            def sb(name, shape, dt):
                return st.enter_context(nc.sbuf_tensor(name, shape, dt))

            def psf(name):
                return st.enter_context(nc.psum_tensor(name, [128, 512], F32))

            def psb(name):
                return st.enter_context(nc.psum_tensor(name, [128, 8, 128], BF16))

            P = Prog(nc)
            h = H(P)

            xT = sb("xT", [128, 8, S], BF16)
            wsl = [sb("wsl%d" % i, [128, 8, 512], BF16) for i in range(7)]
            wB, wC, wH, wQ, wK, wV, wG = wsl
            NXB = 4
            xb = [ycat[:, 4 + i, :].bitcast(F32) for i in range(NXB)]
            for tt_ in range(4):
                P.alias[("ycat", tt_)] = [("xb", i) for i in range(NXB)]
            rows = sb("rows", [32, 128], F32)
            posi = sb("posi", [16, 128], I32)
            posf16 = sb("posf16", [16, 128], F32)
            smallT = sb("smallT", [128, 20], F32)
            posT = sb("posT", [128, 16], F32)
            invf = sb("invf", [128, 64], F32)
            cosF = sb("cosF", [128, 16, 128], F32)
            sinS = sb("sinS", [128, 16, 128], F32)
            xiT = sb("xiT", [128, 4, 128], F32)
            maskT = sb("maskT", [128, 4, 128], F32)
            zeta = sb("zeta", [128, 4], F32)
            cdT = sb("cdT", [128, 512], F32)
            hs = [sb("hs%d" % i, [128, 512], F32) for i in range(2)]
            abuf = [sb("abuf%d" % i, [128, 512], F32) for i in range(2)]
            ang = xb[0].rearrange("p (a c) -> p a c", c=64)
            kkf = xb[1].rearrange("p (a c) -> p a c", c=64)
            kki_t = sb("kki_t", [128, 1024], I32)
            kki = kki_t[:].rearrange("p (a c) -> p a c", c=64)
            R_ANG = ("xb", 0)
            R_KKF = ("xb", 1)
            R_KKI = "kki_t"
            def f4(ap):
                return ap.rearrange("p (h d) -> p h d", h=4)
            def wf(i):
                return wsl[i][:].rearrange("p a b -> p (a b)").bitcast(F32)
            t1 = [f4(wf(0)[:, i * 1024:i * 1024 + 512]) for i in range(2)]
            t2 = [f4(wf(0)[:, i * 1024 + 512:i * 1024 + 1024]) for i in range(2)]
            gsr = [f4(wf(1)[:, i * 512:(i + 1) * 512]) for i in range(2)]
            yn = [f4(wf(1)[:, 1024 + i * 512:1024 + (i + 1) * 512]) for i in range(2)]
            Sst = hs[0]
            Stmp = hs[1]
            sq = f4(abuf[0][:])
            oc = f4(abuf[1][:])
            for i in range(2):
                P.alias[("t1", i)] = [("wslj", 0, j) for j in range(4)]
                P.alias[("t2", i)] = [("wslj", 0, j) for j in range(4)]
                P.alias[("gsr", i)] = [("wslj", 1, j) for j in range(4)]
                P.alias[("yn", i)] = [("wslj", 1, j) for j in range(4)]
            lnstA = sb("lnstA", [128, 4, 16], F32)
            xinA = [wf(2)[:, i * 1024:(i + 1) * 1024] for i in range(2)]
            lnA = wf(5)
            x1bA_all = cosF[:].rearrange("p a b -> p (a b)").bitcast(BF16)
            x1bA = [x1bA_all[:, i * 1024:(i + 1) * 1024] for i in range(4)]
            for i in range(2):
                P.alias[("xinA", i)] = [("wslj", 2, j) for j in range(4)]
            P.alias["ln1wA"] = ("wsl", 5)
            P.alias["ln1bA"] = ("wsl", 5)
            for i in range(4):
                P.alias[("x1bA", i)] = "cosF"
                P.alias[("x1", i)] = [("ubuf", i // 2), ("U", 2 * (i // 2)), ("U", 2 * (i // 2) + 1)]
                P.alias[("U", i)] = ("ubuf", i // 2)
            P.alias["Sst"] = ("hs", 0)
            P.alias["Stmp"] = ("hs", 1)
            P.alias["sq"] = ("ab", 0)
            P.alias["oc"] = ("ab", 1)
            qr = [sb("qr%d" % i, [128, 512], BF16) for i in range(2)]
            kr = [sb("kr%d" % i, [128, 512], BF16) for i in range(4)]
            vb = [sb("vb%d" % i, [128, 512], BF16) for i in range(4)]
            vz = [sb("vz%d" % i, [128, 512], BF16) for i in range(4)]
            qhT = [sb("qhT%d" % i, [128, 4, 128], BF16) for i in range(3)]
            kT = [sb("kT%d" % i, [128, 4, 128], BF16) for i in range(2)]
            PT = [sb("PT%d" % i, [128, 4, 128], BF16) for i in range(2)]
            Sb = [sb("Sb%d" % i, [128, 512], BF16) for i in range(2)]
            ob = [sb("ob%d" % i, [128, 512], BF16) for i in range(2)]
            st4 = sb("st4", [128, 2, 28], F32)

            pf = [psf("pf%d" % i) for i in range(6)]
            pb = [psb("pb%d" % i) for i in range(2)]
            PF = [("ps", i) for i in range(6)]
            PB = [("ps", 6 + i) for i in range(2)]

            P.op("pool", lambda e: e.memset(identf[:], 0.0), [], ["identf"])
            P.op("pool", lambda e: e.affine_select(
                out=identf[:], in_=identf[:], pattern=[[-1, 128]], compare_op=ALU.not_equal,
                fill=1.0, base=0, channel_multiplier=1), ["identf"], ["identf"])
            h.cp("dve", ident[:], identf[:], ["identf"], ["ident"])
            P.op("pool", lambda e: e.memset(neghalf[:], -0.5), [], ["neghalf"])
            P.op("pool", lambda e: e.memset(rows[:], 0.0), [], ["rows"])

            def load_x(blk):
                h.dma("sp", xb[blk % NXB], x[blk * 128:(blk + 1) * 128, :], ("xb", blk % NXB),
                      [], [("xb", blk % NXB)])

            def load_w(i):
                h.dma("pool", wsl[i][:],
                      w_in[:, i * 512:(i + 1) * 512].rearrange("(kc p) n -> p kc n", p=128),
                      ("wsl", i), [], [("wsl", i)])

            def load_wj(i, j):
                h.dma("pool", wsl[i][:, :, j * 128:(j + 1) * 128],
                      w_in[:, i * 512 + j * 128:i * 512 + (j + 1) * 128].rearrange(
                          "(kc p) n -> p kc n", p=128),
                      ("wslj", i, j), [], [("wslj", i, j)])

            load_x(0)
            load_x(1)
            load_x(2)
            load_x(3)
            for j in range(4):
                for i in range(3):
                    load_wj(i, j)
            h.dma("act", rows[0:12, :], conv_w.rearrange("k (j p) -> (k j) p", p=128), "rows",
                  ["rows"], ["rows"])
            h.dma("act", rows[12:16, :], gn_w.rearrange("(h p) -> h p", p=128), "rows",
                  ["rows"], ["rows"])
            h.dma("act", rows[16:20, :], gn_b.rearrange("(h p) -> h p", p=128), "rows",
                  ["rows"], ["rows"])
            h.dma("act", posi[:], pos.rearrange("(b p) -> b p", p=128), "posi", [], ["posi"])
            h.dma("act", invf[:], c_invf.partition_broadcast(128), "invf", [], ["invf"])
            h.dma("act", xiT[:], c_xi.rearrange("(h i) -> h i", h=4).partition_broadcast(128),
                  "xiT", [], ["xiT"])
            h.dma("act", maskT[:], c_mask.rearrange("p (h i) -> p h i", h=4), "maskT", [], ["maskT"])
            h.dma("act", zeta[:], c_zeta, "zeta", [], ["zeta"])
            h.dma("act", cdT[:], c_cd.partition_broadcast(128), "cdT", [], ["cdT"])


            def setup_small():
                h.tr(pf[0][:, 0:20], rows[0:20, :], identf[0:20, 0:20], ["rows", "identf"], [PF[0]])
                h.cp("dve", smallT[:], pf[0][:, 0:20], [PF[0]], ["smallT"])
                h.cp("dve", posf16[:], posi[:], ["posi"], ["posf16"])
                h.tr(pf[1][:, 0:16], posf16[:], identf[0:16, 0:16], ["posf16", "identf"], [PF[1]])
                h.cp("dve", posT[:], pf[1][:, 0:16], [PF[1]], ["posT"])

            cwT = smallT[:, 0:12]
            gnwT = smallT[:, 12:16]
            gnbT = smallT[:, 16:20]

            def setup_rotary():
                h.tt("dve", ang, posT[:].unsqueeze(2).broadcast_to([128, 16, 64]),
                     invf[:].unsqueeze(1).broadcast_to([128, 16, 64]), ALU.mult,
                     ["posT", "invf"], [R_ANG])
                yield

                def reduce_sin(src, srcname, dst_lo, dst_hi, lo_scale, hi_scale, dstname):
                    h.ts("dve", kki, src, float(1.0 / (2 * PI)), ALU.mult, [srcname], [R_KKI])
                    yield
                    h.cp("dve", kkf, kki, [R_KKI], [R_KKF])
                    yield
                    h.stt(kkf, kkf, float(-2 * PI), src, ALU.mult, ALU.add,
                          [R_KKF, srcname], [R_KKF])
                    yield
                    h.ts("dve", kkf, kkf, -PI, ALU.max, [R_KKF], [R_KKF], s2=PI, op1=ALU.min)
                    yield
                    h.act(dst_lo, kkf, AF.Sin, [R_KKF], [dstname], scale=lo_scale)
                    h.act(dst_hi, kkf, AF.Sin, [R_KKF], [dstname], scale=hi_scale)
                    yield

                yield from reduce_sin(ang, R_ANG, sinS[:, :, 0:64], sinS[:, :, 64:128], -1.0, 1.0,
                                      "sinS")
                h.ts("dve", ang, ang, PI / 2.0, ALU.add, [R_ANG], [R_ANG])
                yield
                yield from reduce_sin(ang, R_ANG, cosF[:, :, 0:64], cosF[:, :, 64:128], 1.0, 1.0,
                                      "cosF")


            def x_block(blk):
                r_xb = ("xb", blk % NXB)
                b0 = 2 * (blk % 2)
                for kc in range(8):
                    bank = pf[b0 + kc // 4]
                    h.tr(bank[:, (kc % 4) * 128:(kc % 4 + 1) * 128],
                         xb[blk % NXB][:, kc * 128:(kc + 1) * 128], identf[:],
                         [r_xb, "identf"], [PF[b0 + kc // 4]])
                h.cp("act", xT[:, 0:4, blk * 128:(blk + 1) * 128],
                     pf[b0][:].rearrange("p (a d) -> p a d", a=4), [PF[b0]], [("xT", blk // 4)])
                h.cp("dve", xT[:, 4:8, blk * 128:(blk + 1) * 128],
                     pf[b0 + 1][:].rearrange("p (a d) -> p a d", a=4), [PF[b0 + 1]],
                     [("xT", blk // 4)])
                if blk + NXB < NB:
                    load_x(blk + NXB)

            for blk in range(4):
                x_block(blk)
            setup_small()
            for blk in range(4, NB):
                x_block(blk)
            for i in range(3, 7):
                load_w(i)

            casts = []
            for g in range(2):
                casts.append((s_wout[g], w_out[:, g * 512:(g + 1) * 512].rearrange(
                    "(kc p) n -> p kc n", p=128), ("s_cast_wout", g), ("s_wout", g)))
            casts.append((s_ff1[0], w_ff1[:, 0:512].rearrange(
                "(kc p) n -> p kc n", p=128), "s_cast_ff1_0", ("s_ff1", 0)))
            for g in range(1, 8):
                casts.append((s_ff1[g], w_ff1[:, g * 512:(g + 1) * 512].rearrange(
                    "(kc p) n -> p kc n", p=128), "s_cast", ("s_ff1", g)))
            for g in range(2):
                casts.append((s_gate[g], w_gate[:, g * 512:(g + 1) * 512].rearrange(
                    "(kc p) n -> p kc n", p=128), "s_cast", ("s_gate", g)))
            for c in range(8):
                casts.append((s_ff2[c], w_ff2[:, c * 128:(c + 1) * 128].rearrange(
                    "(fc p) n -> p fc n", p=128), "s_cast", ("s_ff2", c)))

            def issue_casts(cnt):
                for _ in range(cnt):
                    if casts:
                        dst, src, key, res = casts.pop(0)
                        h.dma("pool", dst, src, key, [], [res])

            U = [ubuf[j // 2][:, (j % 2) * 514:(j % 2 + 1) * 514] for j in range(4)]
            rot_gen = setup_rotary()
            it = 0
            for tt in range(4):
                for j in range(4):
                    ub = U[j]
                    r_u = ("U", j)
                    if tt == 0:
                        h.P.op("dve", lambda e, ub=ub: e.memset(ub[:, 0:2], 0.0), [], [r_u])
                    par = it % 2
                    it += 1
                    tok = slice(tt * 512, (tt + 1) * 512)
                    r_xT = ("xT", tt)
                    banks = (pf[par], pf[2 + par], pf[4 + par])
                    rb = (PF[par], PF[2 + par], PF[4 + par])
                    for gi, slab in enumerate((wB, wC, wH)):
                        for kc in range(8):
                            h.mm(banks[gi][:], slab[:, kc, j * 128:(j + 1) * 128], xT[:, kc, tok],
                                 kc == 0, kc == 7, [("wslj", gi, j), r_xT], [rb[gi]])
                    hsb = hs[par]
                    ab = abuf[par]
                    h.cp("act", hsb[:], banks[2][:], [rb[2]], [("hs", par)])
                    u_t = ub[:, 2:514]
                    u_m1 = ub[:, 1:513]
                    u_m2 = ub[:, 0:512]
                    h.tt("dve", u_t, banks[1][:], hsb[:], ALU.mult, [rb[1], ("hs", par)], [r_u])
                    h.act(ab[:], u_t, AF.Copy, [r_u, "smallT"], [("ab", par)],
                          scale=cwT[:, 8 + j:9 + j])
                    h.stt(ab[:], u_m1, cwT[:, 4 + j:5 + j], ab[:], ALU.mult, ALU.add,
                          [r_u, ("ab", par), "smallT"], [("ab", par)])
                    h.stt(ab[:], u_m2, cwT[:, j:j + 1], ab[:], ALU.mult, ALU.add,
                          [r_u, ("ab", par), "smallT"], [("ab", par)])
                    h.tt("dve", ycat[:, j, tok], banks[0][:], ab[:], ALU.mult,
                         [rb[0], ("ab", par)], [("ycat", tt)])
                    if tt < 3:
                        h.cp("dve", ub[:, 0:2], ub[:, 512:514], [r_u], [r_u])
                    if tt >= 1:
                        for _ in range(2):
                            next(rot_gen, None)
                        if j % 2 == 1:
                            issue_casts(1)
            for _ in rot_gen:
                pass

            q_ps, k_ps, v_ps, s_ps, o_ps, kv_ps = pf
            rq, rk, rv, rs, ro, rkv = PF

            def ctok_(n):
                return slice(n * 128, (n + 1) * 128)

            def rx(n):
                return ("xT", n // 4)

            def P1(n):
                for (bank, rbk, slab, si) in ((v_ps, rv, wV, 5), (q_ps, rq, wQ, 3), (k_ps, rk, wK, 4)):
                    for kc in range(8):
                        h.mm(bank[:], xT[:, kc, ctok_(n)], slab[:, kc, :], kc == 0, kc == 7,
                             [("wsl", si), rx(n)], [rbk])

            def E1(n):
                cosb = cosF[:, n, :].unsqueeze(1).broadcast_to([128, 4, 128])
                sin_lo = sinS[:, n, 0:64].unsqueeze(1).broadcast_to([128, 4, 64])
                sin_hi = sinS[:, n, 64:128].unsqueeze(1).broadcast_to([128, 4, 64])
                h.cp("act", vb[n % 4][:], v_ps[:], [rv], [("vb", n % 4)])
                h.tt("dve", f4(vz[n % 4][:]), f4(v_ps[:]),
                     zeta[:].unsqueeze(2).broadcast_to([128, 4, 128]), ALU.mult,
                     [rv, "zeta"], [("vz", n % 4)])
                for ti, (bank, rbk, dst, dname) in enumerate(
                        ((q_ps, rq, qr[n % 2], ("qr", n % 2)), (k_ps, rk, kr[n % 4], ("kr", n % 4)))):
                    b4 = f4(bank[:])
                    a1, a2 = t1[ti], t2[ti]
                    h.tt("dve", a1, b4, cosb, ALU.mult, [rbk, "cosF"], [("t1", ti)])
                    h.tt("dve", a2[:, :, 0:64], b4[:, :, 64:128], sin_lo, ALU.mult,
                         [rbk, "sinS"], [("t2", ti)])
                    h.tt("dve", a2[:, :, 64:128], b4[:, :, 0:64], sin_hi, ALU.mult,
                         [rbk, "sinS"], [("t2", ti)])
                    h.tt("pool", f4(dst[:]), a1, a2, ALU.add, [("t1", ti), ("t2", ti)], [dname])

            def P2(n):
                for hh in range(4):
                    h.tr(pb[0][:, hh, :], qr[n % 2][:, hh * 128:(hh + 1) * 128], ident[:],
                         [("qr", n % 2), "ident"], [PB[0]])
                for hh in range(4):
                    h.tr(pb[0][:, 4 + hh, :], kr[n % 4][:, hh * 128:(hh + 1) * 128], ident[:],
                         [("kr", n % 4), "ident"], [PB[0]])

            def E2(n):
                h.tt("dve", qhT[n % 3][:], pb[0][:, 0:4, :], xiT[:], ALU.mult, [PB[0], "xiT"],
                     [("qhT", n % 3)])
                h.cp("act", kT[n % 2][:], pb[0][:, 4:8, :], [PB[0]], [("kT", n % 2)])

            def P3(n):
                for hh in range(4):
                    h.mm(s_ps[:, hh * 128:(hh + 1) * 128], kT[n % 2][:, hh, :], qhT[n % 3][:, hh, :],
                         True, True, [("kT", n % 2), ("qhT", n % 3)], [rs])

            def E3(n):
                h.tt("dve", PT[n % 2][:], f4(s_ps[:]), maskT[:], ALU.mult, [rs, "maskT"],
                     [("PT", n % 2)])

            def P4(n):
                for hh in range(4):
                    hs_ = slice(hh * 128, (hh + 1) * 128)
                    h.mm(o_ps[:, hs_], PT[n % 2][:, hh, :], vb[n % 4][:, hs_], True, n == 0,
                         [("PT", n % 2), ("vb", n % 4)], [ro])
                    if n > 0:
                        h.mm(o_ps[:, hs_], qhT[n % 3][:, hh, :], Sb[n % 2][:, hs_], False, True,
                             [("qhT", n % 3), ("Sb", n % 2)], [ro])
                if n < NB - 1:
                    for hh in range(4):
                        hs_ = slice(hh * 128, (hh + 1) * 128)
                        h.mm(kv_ps[:, hs_], kr[n % 4][:, hs_], vz[n % 4][:, hs_], True, True,
                             [("kr", n % 4), ("vz", n % 4)], [rkv])

            def E4a(n):
                if n < NB - 1:
                    if n == 0:
                        h.cp("dve", Sst[:], kv_ps[:], [rkv], ["Sst"])
                    else:
                        h.tt("dve", Sst[:], kv_ps[:], Stmp[:], ALU.add, [rkv, "Stmp"], ["Sst"])
                    h.cp("act", Sb[(n + 1) % 2][:], Sst[:], ["Sst"], [("Sb", (n + 1) % 2)])
                    if n + 1 < NB - 1:
                        h.tt("pool", Stmp[:], Sst[:], cdT[:], ALU.mult, ["Sst", "cdT"], ["Stmp"])

            def P4g(n):
                for hh in range(4):
                    for kc in range(8):
                        h.mm(v_ps[:, hh * 128:(hh + 1) * 128], wG[:, kc, hh * 128:(hh + 1) * 128],
                             xT[:, kc, ctok_(n)], kc == 0, kc == 7, [("wsl", 6), rx(n)], [rv])
                h.act(gsr[n % 2], f4(v_ps[:]), AF.Silu, [rv], [("gsr", n % 2)])

            def E4b(n):
                i = n % 2
                sum4 = st4[:, i, 0:4]
                ssq4 = st4[:, i, 4:8]
                mean4 = st4[:, i, 8:12]
                m24 = st4[:, i, 12:16]
                var4 = st4[:, i, 16:20]
                rstd4 = st4[:, i, 20:24]
                nb4 = st4[:, i, 24:28]
                rs4 = ("st4", i)
                o4 = f4(o_ps[:])
                rsa = ("st4a", i)
                rsb = ("st4b", i)
                h.red(sum4, o4, [ro], [rsa])
                h.act(sq, o4, AF.Square, [ro], ["sq"])
                h.red(ssq4, sq, ["sq"], [rsb])
                h.ts("pool", mean4, sum4, 1.0 / 128.0, ALU.mult, [rsa], [rs4], s2=0.0, op1=ALU.add)
                h.tt("pool", m24, mean4, mean4, ALU.mult, [rs4], [rs4])
                h.ts("pool", var4, ssq4, 1.0 / 128.0, ALU.mult, [rsb, rs4], [rs4], s2=GN_EPS,
                     op1=ALU.add)
                h.tt("pool", var4, var4, m24, ALU.subtract, [rs4], [rs4])
                h.tt("pool", rstd4, var4, neghalf[:], ALU.pow, [rs4, "neghalf"], [rs4])
                h.tt("dve", oc, o4, mean4.unsqueeze(2).broadcast_to([128, 4, 128]), ALU.subtract,
                     [ro, rs4], ["oc"])
                h.tt("dve", f4(ob[i][:]), oc, rstd4.unsqueeze(2).broadcast_to([128, 4, 128]),
                     ALU.mult, ["oc", rs4], [("ob", i)])

            def P5(n):
                for hh in range(4):
                    h.tr(pb[1][:, hh, :], ob[n % 2][:, hh * 128:(hh + 1) * 128], ident[:],
                         [("ob", n % 2), "ident"], [PB[1]])

            def E5(n):
                i = n % 2
                for hh in range(4):
                    h.act(yn[i][:, hh, :], pb[1][:, hh, :], AF.Identity, [PB[1], "smallT"],
                          [("yn", i)], scale=gnwT[:, hh:hh + 1], bias=gnbT[:, hh:hh + 1])
                h.tt("pool", ycat[:, 4:8, ctok_(n)], yn[i], gsr[i], ALU.mult,
                     [("yn", i), ("gsr", i)], [("ycat", n // 4)])

            def ok(n):
                return 0 <= n < NB

            gen_lnA = make_gen_ln(P, h, lnstA, neghalf)
            ln1wA = lnA[:, 0:1024]
            ln1bA = lnA[:, 1024:2048]

            def b0_setup():
                h.dma("sp", wsl[3][:], s_wout[0], ("b0k", 0), [("s_wout", 0)], [("wsl", 3)])
                h.dma("sp", wsl[4][:], s_wout[1], ("b0k", 1), [("s_wout", 1)], [("wsl", 4)])
                h.dma("sp", ln1wA, ln1_w.partition_broadcast(128), ("b0l", 0), [], ["ln1wA"])
                h.dma("sp", ln1bA, ln1_b.partition_broadcast(128), ("b0l", 1), [], ["ln1bA"])

            def b0_mix(bi):
                h.dma("sp", xinA[bi % 2], x[bi * 128:(bi + 1) * 128, :], ("xinA", bi % 2),
                      [], [("xinA", bi % 2)])
                bank, rbk = pf[bi % 2], PF[bi % 2]
                for hf in range(2):
                    for kc in range(8):
                        h.mm(bank[:], ycat[:, kc, bi * 128:(bi + 1) * 128], wsl[3 + hf][:, kc, :],
                             kc == 0, kc == 7, [("ycat", 0), ("wsl", 3 + hf)], [rbk])
                    h.stt(x1v(bi)[:, hf * 512:(hf + 1) * 512], xinA[bi % 2][:, hf * 512:(hf + 1) * 512],
                          ALPHA, bank[:], ALU.mult, ALU.add, [("xinA", bi % 2), rbk], [("x1", bi)])

            def gen_b0norm(bi):
                yield from gen_lnA(x1v(bi), ("x1", bi), ln1wA, ln1bA, "ln1wA", "ln1bA", bi)
                h.cp("act", x1bA[bi], x1v(bi), [("x1", bi)], [("x1bA", bi)])
                yield
                yield
                yield
                for kc in range(8):
                    h.tr(pb[0][:, kc, :], x1bA[bi][:, kc * 128:(kc + 1) * 128], ident[:],
                         [("x1bA", bi), "ident"], [PB[0]])
                h.cp("act", ycat[:, :, bi * 128:(bi + 1) * 128], pb[0][:], [PB[0]], [("ycat", 0)])
                yield

            b0_gens = []

            def b0_advance(k):
                for _ in range(k):
                    for g in list(b0_gens):
                        try:
                            next(g)
                        except StopIteration:
                            b0_gens.remove(g)

            P1(0)
            for t in range(NB + 4):
                drain = t >= 15
                if ok(t):
                    E1(t)
                if drain:
                    b0_advance(1)
                if ok(t - 1):
                    P2(t - 1)
                    E2(t - 1)
                if drain:
                    b0_advance(1)
                if ok(t - 2):
                    P3(t - 2)
                    E3(t - 2)
                if drain:
                    b0_advance(1)
                if ok(t - 3):
                    P4g(t - 3)
                    P4(t - 3)
                    E4a(t - 3)
                    E4b(t - 3)
                if drain:
                    b0_advance(2)
                if ok(t - 4):
                    P5(t - 4)
                    E5(t - 4)
                if ok(t + 1):
                    P1(t + 1)
                issue_casts(1)
                if t == 15:
                    b0_setup()
                if t == 17:
                    h.dma("sp", kpre[:], s_ff1[0], "kpre", [("s_ff1", 0)], ["kpre"])
                if t in (15, 16):
                    for bi in (2 * (t - 15), 2 * (t - 15) + 1):
                        b0_mix(bi)
                        b0_gens.append(gen_b0norm(bi))
                if drain:
                    b0_advance(3)
            issue_casts(len(casts))
            while b0_gens:
                b0_advance(1)

            P.join_all()
            P.emit(st, "a", st0)

        with ExitStack() as st:
            def sb(name, shape, dt):
                return st.enter_context(nc.sbuf_tensor(name, shape, dt))

            def psf(name):
                return st.enter_context(nc.psum_tensor(name, [128, 512], F32))

            def psb(name):
                return st.enter_context(nc.psum_tensor(name, [128, 8, 128], BF16))

            P = Prog(nc)
            h = H(P)

            lnt = [sb("lnt%d" % i, [128, D], F32) for i in range(4)]
            ln1w_t, ln1b_t, ln2w_t, ln2b_t = lnt
            wproj = sb("wproj", [128, 2, D], BF16)
            acc2 = sb("acc2", [128, 4, D], F32)
            hT = sb("hT", [128, 32, 512], BF16)
            NK = 3
            NC2 = 3
            kslab = [sb("kslab%d" % i, [128, 8, 512], BF16) for i in range(NK)]
            cslab = [sb("cslab%d" % i, [128, 32, 128], BF16) for i in range(NC2)]
            xin = [sb("xin%d" % i, [128, D], F32) for i in range(2)]
            x1b = [sb("x1b%d" % i, [128, D], BF16) for i in range(4)]
            pbf = [sb("pbf%d" % i, [128, 256], BF16) for i in range(4)]
            pT = sb("pT", [128, 4, 2, 128], BF16)
            sg = [sb("sg%d" % i, [128, 512], F32) for i in range(2)]
            tR = [sb("tR%d" % i, [128, 512], F32) for i in range(2)]
            ffs = [sb("ffs%d" % i, [128, 512], F32) for i in range(2)]
            lnst = sb("lnst", [128, 4, 16], F32)

            pm = [psf("pm%d" % i) for i in range(4)]
            pF = [psf("pF%d" % i) for i in range(2)]
            pT16 = psb("pT16")
            pT32 = psf("pT32")
            RM = [("ps", i) for i in range(4)]
            RF = [("ps", 4 + i) for i in range(2)]
            RT16 = ("ps", 6)
            RT32 = ("ps", 7)


            kcnt = [0]
            ccnt = [0]

            def load_k(src):
                i = kcnt[0] % NK
                kcnt[0] += 1
                h.dma("sp", kslab[i][:], src, ("kslab", i), [], [("kslab", i)])
                return kslab[i], ("kslab", i)

            def load_c(src):
                i = ccnt[0] % NC2
                ccnt[0] += 1
                h.dma("sp", cslab[i][:], src, ("cslab", i), [], [("cslab", i)])
                return cslab[i], ("cslab", i)

            fcnt = [0]

            def next_F():
                i = fcnt[0] % 2
                fcnt[0] += 1
                return pF[i], RF[i], i

            gen_ln = make_gen_ln(P, h, lnst, neghalf)

            def drain(g):
                for _ in g:
                    pass

            def drain_rr(gens):
                gens = list(gens)
                while gens:
                    for g in list(gens):
                        try:
                            next(g)
                        except StopIteration:
                            gens.remove(g)

            bgq = []

            def bg_step(k=1):
                for _ in range(k):
                    while bgq:
                        try:
                            next(bgq[0])
                            break
                        except StopIteration:
                            bgq.pop(0)

            def bg_flush():
                while bgq:
                    drain(bgq.pop(0))

            def B_mix_block(tt, bi, slabs):
                blk = tt * 4 + bi
                h.dma("sp", xin[bi % 2][:], x[blk * 128:(blk + 1) * 128, :], ("xin", bi % 2),
                      [], [("xin", bi % 2)])
                for hf in range(2):
                    slab, rsl = slabs[hf]
                    bk = (2 * bi + hf) % 4
                    for kc in range(8):
                        h.mm(pm[bk][:], ycat[:, kc, blk * 128:(blk + 1) * 128], slab[:, kc, :],
                             kc == 0, kc == 7, [("ycat", tt), rsl], [RM[bk]])
                    h.stt(x1v(bi)[:, hf * 512:(hf + 1) * 512], xin[bi % 2][:, hf * 512:(hf + 1) * 512],
                          ALPHA, pm[bk][:], ALU.mult, ALU.add, [("xin", bi % 2), RM[bk]],
                          [("x1", bi)])

            def phase_B_mix(tt):
                slabs = [load_k(s_wout[0]), load_k(s_wout[1])]
                for bi in range(4):
                    B_mix_block(tt, bi, slabs)

            def gen_bnorm(tt, bi):
                yield from gen_ln(x1v(bi), ("x1", bi), ln1w_t[:], ln1b_t[:], "ln1w", "ln1b", bi)
                h.cp("act", x1b[bi][:], x1v(bi), [("x1", bi)], [("x1b", bi)])
                yield
                yield
                yield
                for kc in range(8):
                    h.tr(pT16[:, kc, :], x1b[bi][:, kc * 128:(kc + 1) * 128], ident[:],
                         [("x1b", bi), "ident"], [RT16])
                h.cp("act", ycat[:, :, (tt * 4 + bi) * 128:(tt * 4 + bi + 1) * 128], pT16[:],
                     [RT16], [("ycat", tt)])
                yield

            def phase_B_norm(tt):
                drain_rr(gen_bnorm(tt, bi) for bi in range(4))

            def phase_ff1(tt):
                tok = slice(tt * 512, (tt + 1) * 512)
                for g in range(8):
                    if tt == 0 and g == 0:
                        slab, rsl = kpre, "kpre"
                    else:
                        slab, rsl = load_k(s_ff1[g])
                    for f4 in range(4):
                        fc = g * 4 + f4
                        bank, rbk, i = next_F()
                        for kc in range(8):
                            h.mm(bank[:], slab[:, kc, f4 * 128:(f4 + 1) * 128], ycat[:, kc, tok],
                                 kc == 0, kc == 7, [rsl, ("ycat", tt)], [rbk])
                        h.act(tR[i][:], bank[:], AF.Relu, [rbk], [("tR", i)])
                        h.tt("dve", hT[:, fc, :], bank[:], tR[i][:], ALU.mult, [rbk, ("tR", i)],
                             [("hT", fc)])
                        bg_step(2)

            def load_p(tt, bi):
                blk = tt * 4 + bi
                h.dma("pool", pbf[bi][:], p_in[blk * 128:(blk + 1) * 128, :], ("pbf", bi),
                      [], [("pbf", bi)])

            def phase_gate_ple(tt):
                for bi in range(4):
                    for k2 in range(2):
                        h.tr(pT16[:, k2, :], pbf[bi][:, k2 * 128:(k2 + 1) * 128], ident[:],
                             [("pbf", bi), "ident"], [RT16])
                    h.cp("act", pT[:, bi, :, :], pT16[:, 0:2, :], [RT16], [("pT", bi)])
                slabs = [load_k(s_gate[0]), load_k(s_gate[1])]
                cnt = 0
                for hf in range(2):
                    slab, rsl = slabs[hf]
                    cols = slice(hf * 512, (hf + 1) * 512)
                    for bi in range(4):
                        for kc in range(8):
                            h.mm(pm[bi][:], ycat[:, kc, (tt * 4 + bi) * 128:(tt * 4 + bi + 1) * 128],
                                 slab[:, kc, :], kc == 0, kc == 7, [("ycat", tt), rsl], [RM[bi]])
                        i = cnt % 2
                        cnt += 1
                        h.act(sg[i][:], pm[bi][:], AF.Sigmoid, [RM[bi]], [("sg", i)])
                        bank, rbk, _ = next_F()
                        for k2 in range(2):
                            h.mm(bank[:], pT[:, bi, k2, :], wproj[:, k2, cols], k2 == 0, k2 == 1,
                                 [("pT", bi), "wproj"], [rbk])
                        h.tt("dve", acc2[:, bi, cols], bank[:], sg[i][:], ALU.mult,
                             [rbk, ("sg", i)], [("acc2", bi)])
                        h.stt(acc2[:, bi, cols], x1v(bi)[:, cols], ALPHA, acc2[:, bi, cols],
                              ALU.mult, ALU.add, [("x1", bi), ("acc2", bi)], [("acc2", bi)])

            def phase_ff2(tt):
                a4 = acc2[:]
                pend = None

                def fin(c, i):
                    for bi in range(4):
                        h.tr(pT32[:, bi * 128:(bi + 1) * 128], ffs[i][:, bi * 128:(bi + 1) * 128],
                             identf[:], [("ffs", i), "identf"], [RT32])
                    h.tt("dve", a4[:, :, c * 128:(c + 1) * 128],
                         pT32[:].rearrange("p (b d) -> p b d", b=4), a4[:, :, c * 128:(c + 1) * 128],
                         ALU.add, [RT32] + [("acc2", b) for b in range(4)],
                         [("acc2", b) for b in range(4)])

                for c in range(8):
                    slab, rsl = load_c(s_ff2[c])
                    bank, rbk, i = next_F()
                    for fc in range(32):
                        h.mm(bank[:], slab[:, fc, :], hT[:, fc, :], fc == 0, fc == 31,
                             [rsl, ("hT", fc)], [rbk])
                        if fc % 8 == 7:
                            bg_step(3)
                        if fc == 7 and pend is not None:
                            fin(*pend)
                            pend = None
                    h.cp("act", ffs[i][:], bank[:], [rbk], [("ffs", i)])
                    pend = (c, i)
                fin(*pend)

            def gen_out(tt, bi):
                blk = tt * 4 + bi
                last = (tt == 3)
                yield from gen_ln(acc2[:, bi, :], ("acc2", bi), ln2w_t[:], ln2b_t[:], "ln2w", "ln2b", bi,
                                  bias_eng=("pool" if (last and bi % 2 == 1) else "dve"))
                h.dma("sp" if last else "pool", out[blk * 128:(blk + 1) * 128, :], acc2[:, bi, :],
                      ("outq", bi), [("acc2", bi)], [("out", blk)])
                yield

            h.dma("act", ln1w_t[:], ln1_w.partition_broadcast(128), "lnt0", [], ["ln1w"])
            h.dma("act", ln1b_t[:], ln1_b.partition_broadcast(128), "lnt1", [], ["ln1b"])
            h.dma("pool", wproj[:], w_proj.rearrange("(kc p) n -> p kc n", p=128), "wproj", [],
                  ["wproj"])
            h.dma("act", ln2w_t[:], ln2_w.partition_broadcast(128), "lnt2", [], ["ln2w"])
            h.dma("act", ln2b_t[:], ln2_b.partition_broadcast(128), "lnt3", [], ["ln2b"])
            for tt in range(4):
                if tt > 0:
                    for bi in range(4):
                        bgq.append(gen_out(tt - 1, bi))
                for bi in range(4):
                    load_p(tt, bi)
                phase_ff1(tt)
                bg_flush()
                phase_gate_ple(tt)
                if tt < 3:
                    phase_B_mix(tt + 1)
                    for bi in range(4):
                        bgq.append(gen_bnorm(tt + 1, bi))
                phase_ff2(tt)
                bg_flush()
            drain_rr(gen_out(3, bi) for bi in range(4))
            P.join_all()
            P.emit(st, "b", st0)
    return nc


def _consts():
    hh = np.arange(4, dtype=np.float64)
    log_g = np.log1p(-np.exp2(-5.0 - hh))
    idx = np.arange(128, dtype=np.float64)
    invf = (10000.0 ** (-np.arange(64, dtype=np.float32) * 2.0 / 128)).astype(np.float32)
    xi = np.exp((idx + 1.0)[None, :] * log_g[:, None])
    mj = np.exp(-(idx + 1.0)[:, None] * log_g[None, :]) * (128.0 ** -0.5)
    causal = (idx[None, :] >= idx[:, None]).astype(np.float64)
    mask = mj[:, :, None] * causal[:, None, :]
    zeta = np.exp((127.0 - idx)[:, None] * log_g[None, :]) * (128.0 ** -0.5)
    cd = np.repeat(np.exp(128.0 * log_g), 128)
    return dict(
        c_invf=invf,
        c_xi=xi.reshape(512).astype(np.float32),
        c_mask=mask.reshape(128, 512).astype(np.float32),
        c_zeta=zeta.astype(np.float32),
        c_cd=cd.astype(np.float32),
    )


_NC_CACHE = {}


def kernel(x, p, positions, w_in, conv_w, ret_gn_w, ret_gn_b, w_out, ln1_w, ln1_b,
           w_ff1, w_ff2, w_ple_gate, w_ple_proj, ln2_w, ln2_b):
    n = 8
    f = lambda a: np.ascontiguousarray(np.asarray(a))
    x = f(x); p = f(p); positions = f(positions)
    shared = dict(
        w_in=f(w_in)[0], conv_w=f(conv_w)[0], gn_w=f(ret_gn_w)[0], gn_b=f(ret_gn_b)[0],
        w_out=f(w_out)[0], ln1_w=f(ln1_w)[0], ln1_b=f(ln1_b)[0], w_ff1=f(w_ff1)[0],
        w_ff2=f(w_ff2)[0], w_gate=f(w_ple_gate)[0], w_proj=f(w_ple_proj)[0],
        ln2_w=f(ln2_w)[0], ln2_b=f(ln2_b)[0],
    )
    shared.update(_consts())
    if "nc" not in _NC_CACHE:
        _NC_CACHE["nc"] = build_nc()
    nc = _NC_CACHE["nc"]
    in_maps = []
    for b in range(n):
        m = dict(shared)
        m["x"] = f(x[b])
        m["p"] = f(p[0, b])
        m["pos"] = f(positions[b]).astype(np.int32)
        in_maps.append(m)
    res = run_bass_kernel_spmd(nc, in_maps, core_ids=list(range(n)))
    return np.stack([np.asarray(r["out"]) for r in res.results], axis=0).astype(np.float32)
```

```python
from contextlib import ExitStack
import numpy as np
import concourse.bass as bass
import concourse.mybir as mybir
from concourse.bass_utils import run_bass_kernel_spmd

F32 = mybir.dt.float32
BF16 = mybir.dt.bfloat16
I32 = mybir.dt.int32
AF = mybir.ActivationFunctionType
ALU = mybir.AluOpType
AX = mybir.AxisListType

ENGS = ("pe", "act", "dve", "pool", "sp")
STRICT_SYNC = True

S = 2048
D = 1024
NB = 16
ALPHA = float(2.0 ** 0.25)
LN_EPS = 1e-5
GN_EPS = 1e-5
PI = float(np.pi)


class _Op:
    __slots__ = ("eng", "fn", "deps", "is_dma", "sem_key", "signals", "sig_val", "idx")


def _excl(r):
    return isinstance(r, tuple) and r[0] == "ps"


class Prog:
    def __init__(self, nc):
        self.nc = nc
        self.ops = []
        self.last_writer = {}
        self.readers = {}
        self.written = set()
        self.alias = {}
        self._alias_seen = set()

    def _add(self, eng, fn, reads, writes, is_dma=False, sem_key=None):
        extra = []
        for r in list(reads) + list(writes):
            if r in self.alias and r not in self._alias_seen:
                self._alias_seen.add(r)
                pa = self.alias[r]
                extra.extend(pa if isinstance(pa, list) else [pa])
        if extra:
            writes = list(writes) + extra
        o = _Op()
        o.eng, o.fn, o.is_dma, o.sem_key = eng, fn, is_dma, sem_key
        o.signals = is_dma
        o.sig_val = 0
        o.idx = len(self.ops)
        deps = {}
        for r in reads:
            w = self.last_writer.get(r)
            if w is not None:
                deps[w] = True
            if _excl(r):
                rd = self.readers.get(r)
                if rd:
                    for x in rd.values():
                        deps.setdefault(x, False)
        for r in writes:
            w = self.last_writer.get(r)
            if w is not None:
                deps.setdefault(w, False)
            rd = self.readers.get(r)
            if rd:
                for x in rd.values():
                    deps.setdefault(x, False)
        o.deps = deps
        for r in reads:
            d = self.readers.setdefault(r, {})
            key = ("dma", o.idx) if is_dma else eng
            d[key] = o.idx
        for r in writes:
            self.last_writer[r] = o.idx
            self.readers[r] = {}
            self.written.add(r)
        self.ops.append(o)
        return o

    def op(self, eng, fn, reads=(), writes=()):
        return self._add(eng, fn, reads, writes)

    def dma(self, eng, fn, sem_key, reads=(), writes=()):
        return self._add(eng, fn, reads, writes, is_dma=True, sem_key=sem_key)

    def wait_all(self, eng, resources):
        return self._add(eng, None, list(resources), [])

    def join_all(self):
        res = list(self.written)
        last = {}
        for o in self.ops:
            if o.fn is not None:
                last[o.eng] = o.idx
        for e in ENGS:
            o = self._add(e, None, res, [])
            for le, li in last.items():
                if le != e:
                    o.deps.setdefault(li, True)

    def emit(self, stack, tag, sem_stack=None):
        nc = self.nc
        sem_stack = sem_stack or stack
        ops = self.ops
        for o in ops:
            nd = set()
            for d, raw in o.deps.items():
                p = ops[d]
                if p.fn is None or d == o.idx:
                    continue
                if (not p.is_dma) and p.eng == o.eng:
                    if o.eng == "pe" or not (raw or STRICT_SYNC):
                        continue
                nd.add(d)
            o.deps = nd
            for d in nd:
                ops[d].signals = True
        esem = {e: sem_stack.enter_context(nc.semaphore("sem_%s_%s" % (tag, e))) for e in ENGS}
        dsem = {}
        ecount = {e: 0 for e in ENGS}
        dcount = {}
        for o in ops:
            if o.is_dma:
                if o.sem_key not in dsem:
                    dsem[o.sem_key] = sem_stack.enter_context(
                        nc.semaphore("dsem_%s_%d" % (tag, len(dsem))))
                    dcount[o.sem_key] = 0
                dcount[o.sem_key] += 16
                o.sig_val = dcount[o.sem_key]
            elif o.signals and o.fn is not None:
                ecount[o.eng] += 1
                o.sig_val = ecount[o.eng]

        def sem_of(p):
            return dsem[p.sem_key] if p.is_dma else esem[p.eng]

        block = stack.enter_context(nc.Block())

        def run_engine(ename):
            def body(e):
                waited = {}
                for o in ops:
                    if o.eng != ename:
                        continue
                    need = {}
                    for d in o.deps:
                        p = ops[d]
                        s = sem_of(p)
                        k = id(s)
                        if k not in need or need[k][1] < p.sig_val:
                            need[k] = (s, p.sig_val)
                    for k, (s, v) in need.items():
                        if waited.get(k, 0) >= v:
                            continue
                        e.wait_ge(s, v)
                        waited[k] = v
                    if o.fn is None:
                        continue
                    inst = o.fn(e)
                    if o.is_dma:
                        inst.then_inc(dsem[o.sem_key], 16)
                    elif o.signals:
                        inst.then_inc(esem[ename], 1)
            return body

        block.tensor(run_engine("pe"))
        block.scalar(run_engine("act"))
        block.vector(run_engine("dve"))
        block.gpsimd(run_engine("pool"))
        block.sync(run_engine("sp"))


class H:
    def __init__(self, P):
        self.P = P

    def mm(self, out, lhsT, rhs, start, stop, reads, writes):
        self.P.op("pe", lambda e: e.matmul(out, lhsT=lhsT, rhs=rhs, start=start, stop=stop),
                  reads, writes)

    def tr(self, out, in_, ident, reads, writes):
        self.P.op("pe", lambda e: e.transpose(out=out, in_=in_, identity=ident), reads, writes)

    def tt(self, eng, out, in0, in1, op, reads, writes):
        self.P.op(eng, lambda e: e.tensor_tensor(out=out, in0=in0, in1=in1, op=op), reads, writes)

    def ts(self, eng, out, in0, s1, op0, reads, writes, s2=None, op1=None):
        if op1 is None:
            self.P.op(eng, lambda e: e.tensor_scalar(out=out, in0=in0, scalar1=s1, scalar2=None,
                                                     op0=op0), reads, writes)
        else:
            self.P.op(eng, lambda e: e.tensor_scalar(out=out, in0=in0, scalar1=s1, scalar2=s2,
                                                     op0=op0, op1=op1), reads, writes)

    def stt(self, out, in0, scalar, in1, op0, op1, reads, writes):
        self.P.op("dve", lambda e: e.scalar_tensor_tensor(out=out, in0=in0, scalar=scalar, in1=in1,
                                                          op0=op0, op1=op1), reads, writes)

    def act(self, out, in_, func, reads, writes, scale=None, bias=None):
        kw = {}
        if scale is not None:
            kw["scale"] = scale
        if bias is not None:
            kw["bias"] = bias
        self.P.op("act", lambda e: e.activation(out=out, in_=in_, func=func, **kw), reads, writes)

    def cp(self, eng, out, in_, reads, writes):
        if eng == "act":
            self.act(out, in_, AF.Copy, reads, writes)
        else:
            self.P.op(eng, lambda e: e.tensor_copy(out=out, in_=in_), reads, writes)

    def red(self, out, in_, reads, writes):
        self.P.op("dve", lambda e: e.tensor_reduce(out=out, in_=in_, op=ALU.add, axis=AX.X),
                  reads, writes)

    def dma(self, eng, out, in_, key, reads, writes, **kw):
        self.P.dma(eng, lambda e: e.dma_start(out=out, in_=in_, **kw), key, reads, writes)


def make_gen_ln(P, h, lnst, neghalf):
    def gen_ln(buf, rbuf, w_ap, b_ap, rw, rb, slot):
        st6 = lnst[:, slot, 0:12]
        mv = lnst[:, slot, 12:14]
        rstd = lnst[:, slot, 14:15]
        nbias = lnst[:, slot, 15:16]
        rl = ("lnst", slot)
        for hf in range(2):
            P.op("dve", lambda e, hf=hf: e.bn_stats(out=st6[:, hf * 6:(hf + 1) * 6],
                                                    in_=buf[:, hf * 512:(hf + 1) * 512]),
                 [rbuf], [rl])
            yield
        P.op("dve", lambda e: e.bn_aggr(out=mv, in_=st6), [rl], [rl])
        yield
        h.ts("dve", rstd, mv[:, 1:2], LN_EPS, ALU.add, [rl], [rl])
        yield
        h.tt("pool", rstd, rstd, neghalf[:, 0:1], ALU.pow, [rl, "neghalf"], [rl])
        yield
        yield
        h.stt(nbias, mv[:, 0:1], -1.0, rstd, ALU.mult, ALU.mult, [rl], [rl])
        yield
        h.act(buf, buf, AF.Identity, [rbuf, rl], [rbuf], scale=rstd, bias=nbias)
        yield
        yield
        h.tt("dve", buf, buf, w_ap, ALU.mult, [rbuf, rw], [rbuf])
        yield
        h.tt("dve", buf, buf, b_ap, ALU.add, [rbuf, rb], [rbuf])
        yield
    return gen_ln


def build_nc():
    nc = bass.Bass("TRN2", target_bir_lowering=False)

    def din(name, shape, dt=F32):
        return nc.dram_tensor(name, shape, dt, kind="ExternalInput").ap()

    x = din("x", [S, D])
    p_in = din("p", [S, 256])
    pos = din("pos", [S], I32)
    w_in = din("w_in", [D, 3584])
    conv_w = din("conv_w", [3, 512])
    gn_w = din("gn_w", [512])
    gn_b = din("gn_b", [512])
    w_out = din("w_out", [D, D])
    ln1_w = din("ln1_w", [D])
    ln1_b = din("ln1_b", [D])
    w_ff1 = din("w_ff1", [D, 4096])
    w_ff2 = din("w_ff2", [4096, D])
    w_gate = din("w_gate", [D, D])
    w_proj = din("w_proj", [256, D])
    ln2_w = din("ln2_w", [D])
    ln2_b = din("ln2_b", [D])
    c_invf = din("c_invf", [64])
    c_xi = din("c_xi", [512])
    c_mask = din("c_mask", [128, 512])
    c_zeta = din("c_zeta", [128, 4])
    c_cd = din("c_cd", [512])
    out = nc.dram_tensor("out", [S, D], F32, kind="ExternalOutput").ap()

    s_ff1 = nc.dram_tensor("s_ff1", [8, 128, 8, 512], BF16).ap()
    s_ff2 = nc.dram_tensor("s_ff2", [8, 128, 32, 128], BF16).ap()
    s_wout = nc.dram_tensor("s_wout", [2, 128, 8, 512], BF16).ap()
    s_gate = nc.dram_tensor("s_gate", [2, 128, 8, 512], BF16).ap()

    with ExitStack() as st0:
        def sb0(name, shape, dt):
            return st0.enter_context(nc.sbuf_tensor(name, shape, dt))

        ycat = sb0("ycat", [128, 8, S], BF16)
        ident = sb0("ident", [128, 128], BF16)
        identf = sb0("identf", [128, 128], F32)
        neghalf = sb0("neghalf", [128, 4], F32)
        ubuf = [sb0("ubuf%d" % i, [128, 2 + S], F32) for i in range(2)]

        def x1v(bi):
            return ubuf[bi // 2][:, (bi % 2) * 1024:(bi % 2 + 1) * 1024]
        kpre = sb0("kpre", [128, 8, 512], BF16)

        with ExitStack() as st:
            def sb(name, shape, dt):
                return st.enter_context(nc.sbuf_tensor(name, shape, dt))

            def psf(name):
                return st.enter_context(nc.psum_tensor(name, [128, 512], F32))

            def psb(name):
                return st.enter_context(nc.psum_tensor(name, [128, 8, 128], BF16))

            P = Prog(nc)
            h = H(P)

            xT = sb("xT", [128, 8, S], BF16)
            wsl = [sb("wsl%d" % i, [128, 8, 512], BF16) for i in range(7)]
            wB, wC, wH, wQ, wK, wV, wG = wsl
            NXB = 4
            xb = [ycat[:, 4 + i, :].bitcast(F32) for i in range(NXB)]
            for tt_ in range(4):
                P.alias[("ycat", tt_)] = [("xb", i) for i in range(NXB)]
            rows = sb("rows", [32, 128], F32)
            posi = sb("posi", [16, 128], I32)
            posf16 = sb("posf16", [16, 128], F32)
            smallT = sb("smallT", [128, 20], F32)
            posT = sb("posT", [128, 16], F32)
            invf = sb("invf", [128, 64], F32)
            cosF = sb("cosF", [128, 16, 128], F32)
            sinS = sb("sinS", [128, 16, 128], F32)
            xiT = sb("xiT", [128, 4, 128], F32)
            maskT = sb("maskT", [128, 4, 128], F32)
            zeta = sb("zeta", [128, 4], F32)
            cdT = sb("cdT", [128, 512], F32)
            hs = [sb("hs%d" % i, [128, 512], F32) for i in range(2)]
            abuf = [sb("abuf%d" % i, [128, 512], F32) for i in range(2)]
            ang = xb[0].rearrange("p (a c) -> p a c", c=64)
            kkf = xb[1].rearrange("p (a c) -> p a c", c=64)
            kki_t = sb("kki_t", [128, 1024], I32)
            kki = kki_t[:].rearrange("p (a c) -> p a c", c=64)
            R_ANG = ("xb", 0)
            R_KKF = ("xb", 1)
            R_KKI = "kki_t"
            def f4(ap):
                return ap.rearrange("p (h d) -> p h d", h=4)
            def wf(i):
                return wsl[i][:].rearrange("p a b -> p (a b)").bitcast(F32)
            t1 = [f4(wf(0)[:, i * 1024:i * 1024 + 512]) for i in range(2)]
            t2 = [f4(wf(0)[:, i * 1024 + 512:i * 1024 + 1024]) for i in range(2)]
            gsr = [f4(wf(1)[:, i * 512:(i + 1) * 512]) for i in range(2)]
            yn = [f4(wf(1)[:, 1024 + i * 512:1024 + (i + 1) * 512]) for i in range(2)]
            Sst = hs[0]
            Stmp = hs[1]
            sq = f4(abuf[0][:])
            oc = f4(abuf[1][:])
            for i in range(2):
                P.alias[("t1", i)] = [("wslj", 0, j) for j in range(4)]
                P.alias[("t2", i)] = [("wslj", 0, j) for j in range(4)]
                P.alias[("gsr", i)] = [("wslj", 1, j) for j in range(4)]
                P.alias[("yn", i)] = [("wslj", 1, j) for j in range(4)]
            lnstA = sb("lnstA", [128, 4, 16], F32)
            xinA = [wf(2)[:, i * 1024:(i + 1) * 1024] for i in range(2)]
            lnA = wf(5)
            x1bA_all = cosF[:].rearrange("p a b -> p (a b)").bitcast(BF16)
            x1bA = [x1bA_all[:, i * 1024:(i + 1) * 1024] for i in range(4)]
            for i in range(2):
                P.alias[("xinA", i)] = [("wslj", 2, j) for j in range(4)]
            P.alias["ln1wA"] = ("wsl", 5)
            P.alias["ln1bA"] = ("wsl", 5)
            for i in range(4):
                P.alias[("x1bA", i)] = "cosF"
                P.alias[("x1", i)] = [("ubuf", i // 2), ("U", 2 * (i // 2)), ("U", 2 * (i // 2) + 1)]
                P.alias[("U", i)] = ("ubuf", i // 2)
            P.alias["Sst"] = ("hs", 0)
            P.alias["Stmp"] = ("hs", 1)
            P.alias["sq"] = ("ab", 0)
            P.alias["oc"] = ("ab", 1)
            qr = [sb("qr%d" % i, [128, 512], BF16) for i in range(2)]
            kr = [sb("kr%d" % i, [128, 512], BF16) for i in range(4)]
            vb = [sb("vb%d" % i, [128, 512], BF16) for i in range(4)]
            vz = [sb("vz%d" % i, [128, 512], BF16) for i in range(4)]
            qhT = [sb("qhT%d" % i, [128, 4, 128], BF16) for i in range(3)]
            kT = [sb("kT%d" % i, [128, 4, 128], BF16) for i in range(2)]
            PT = [sb("PT%d" % i, [128, 4, 128], BF16) for i in range(2)]
            Sb = [sb("Sb%d" % i, [128, 512], BF16) for i in range(2)]
            ob = [sb("ob%d" % i, [128, 512], BF16) for i in range(2)]
            st4 = sb("st4", [128, 2, 28], F32)

            pf = [psf("pf%d" % i) for i in range(6)]
            pb = [psb("pb%d" % i) for i in range(2)]
            PF = [("ps", i) for i in range(6)]
            PB = [("ps", 6 + i) for i in range(2)]

            P.op("pool", lambda e: e.memset(identf[:], 0.0), [], ["identf"])
            P.op("pool", lambda e: e.affine_select(
                out=identf[:], in_=identf[:], pattern=[[-1, 128]], compare_op=ALU.not_equal,
                fill=1.0, base=0, channel_multiplier=1), ["identf"], ["identf"])
            h.cp("dve", ident[:], identf[:], ["identf"], ["ident"])
            P.op("pool", lambda e: e.memset(neghalf[:], -0.5), [], ["neghalf"])
            P.op("pool", lambda e: e.memset(rows[:], 0.0), [], ["rows"])

            def load_x(blk):
                h.dma("sp", xb[blk % NXB], x[blk * 128:(blk + 1) * 128, :], ("xb", blk % NXB),
                      [], [("xb", blk % NXB)])

            def load_w(i):
                h.dma("pool", wsl[i][:],
                      w_in[:, i * 512:(i + 1) * 512].rearrange("(kc p) n -> p kc n", p=128),
                      ("wsl", i), [], [("wsl", i)])

            def load_wj(i, j):
                h.dma("pool", wsl[i][:, :, j * 128:(j + 1) * 128],
                      w_in[:, i * 512 + j * 128:i * 512 + (j + 1) * 128].rearrange(
                          "(kc p) n -> p kc n", p=128),
                      ("wslj", i, j), [], [("wslj", i, j)])

            load_x(0)
            load_x(1)
            load_x(2)
            load_x(3)
            for j in range(4):
                for i in range(3):
                    load_wj(i, j)
            h.dma("act", rows[0:12, :], conv_w.rearrange("k (j p) -> (k j) p", p=128), "rows",
                  ["rows"], ["rows"])
            h.dma("act", rows[12:16, :], gn_w.rearrange("(h p) -> h p", p=128), "rows",
                  ["rows"], ["rows"])
            h.dma("act", rows[16:20, :], gn_b.rearrange("(h p) -> h p", p=128), "rows",
                  ["rows"], ["rows"])
            h.dma("act", posi[:], pos.rearrange("(b p) -> b p", p=128), "posi", [], ["posi"])
            h.dma("act", invf[:], c_invf.partition_broadcast(128), "invf", [], ["invf"])
            h.dma("act", xiT[:], c_xi.rearrange("(h i) -> h i", h=4).partition_broadcast(128),
                  "xiT", [], ["xiT"])
            h.dma("act", maskT[:], c_mask.rearrange("p (h i) -> p h i", h=4), "maskT", [], ["maskT"])
            h.dma("act", zeta[:], c_zeta, "zeta", [], ["zeta"])
            h.dma("act", cdT[:], c_cd.partition_broadcast(128), "cdT", [], ["cdT"])


            def setup_small():
                h.tr(pf[0][:, 0:20], rows[0:20, :], identf[0:20, 0:20], ["rows", "identf"], [PF[0]])
                h.cp("dve", smallT[:], pf[0][:, 0:20], [PF[0]], ["smallT"])
                h.cp("dve", posf16[:], posi[:], ["posi"], ["posf16"])
                h.tr(pf[1][:, 0:16], posf16[:], identf[0:16, 0:16], ["posf16", "identf"], [PF[1]])
                h.cp("dve", posT[:], pf[1][:, 0:16], [PF[1]], ["posT"])

            cwT = smallT[:, 0:12]
            gnwT = smallT[:, 12:16]
            gnbT = smallT[:, 16:20]

            def setup_rotary():
                h.tt("dve", ang, posT[:].unsqueeze(2).broadcast_to([128, 16, 64]),
                     invf[:].unsqueeze(1).broadcast_to([128, 16, 64]), ALU.mult,
                     ["posT", "invf"], [R_ANG])
                yield

                def reduce_sin(src, srcname, dst_lo, dst_hi, lo_scale, hi_scale, dstname):
                    h.ts("dve", kki, src, float(1.0 / (2 * PI)), ALU.mult, [srcname], [R_KKI])
                    yield
                    h.cp("dve", kkf, kki, [R_KKI], [R_KKF])
                    yield
                    h.stt(kkf, kkf, float(-2 * PI), src, ALU.mult, ALU.add,
                          [R_KKF, srcname], [R_KKF])
                    yield
                    h.ts("dve", kkf, kkf, -PI, ALU.max, [R_KKF], [R_KKF], s2=PI, op1=ALU.min)
                    yield
                    h.act(dst_lo, kkf, AF.Sin, [R_KKF], [dstname], scale=lo_scale)
                    h.act(dst_hi, kkf, AF.Sin, [R_KKF], [dstname], scale=hi_scale)
                    yield

                yield from reduce_sin(ang, R_ANG, sinS[:, :, 0:64], sinS[:, :, 64:128], -1.0, 1.0,
                                      "sinS")
                h.ts("dve", ang, ang, PI / 2.0, ALU.add, [R_ANG], [R_ANG])
                yield
                yield from reduce_sin(ang, R_ANG, cosF[:, :, 0:64], cosF[:, :, 64:128], 1.0, 1.0,
                                      "cosF")


            def x_block(blk):
                r_xb = ("xb", blk % NXB)
                b0 = 2 * (blk % 2)
                for kc in range(8):
                    bank = pf[b0 + kc // 4]
                    h.tr(bank[:, (kc % 4) * 128:(kc % 4 + 1) * 128],
                         xb[blk % NXB][:, kc * 128:(kc + 1) * 128], identf[:],
                         [r_xb, "identf"], [PF[b0 + kc // 4]])
                h.cp("act", xT[:, 0:4, blk * 128:(blk + 1) * 128],
                     pf[b0][:].rearrange("p (a d) -> p a d", a=4), [PF[b0]], [("xT", blk // 4)])
                h.cp("dve", xT[:, 4:8, blk * 128:(blk + 1) * 128],
                     pf[b0 + 1][:].rearrange("p (a d) -> p a d", a=4), [PF[b0 + 1]],
                     [("xT", blk // 4)])
                if blk + NXB < NB:
                    load_x(blk + NXB)

            for blk in range(4):
                x_block(blk)
            setup_small()
            for blk in range(4, NB):
                x_block(blk)
            for i in range(3, 7):
                load_w(i)

            casts = []
            for g in range(2):
                casts.append((s_wout[g], w_out[:, g * 512:(g + 1) * 512].rearrange(
                    "(kc p) n -> p kc n", p=128), ("s_cast_wout", g), ("s_wout", g)))
            casts.append((s_ff1[0], w_ff1[:, 0:512].rearrange(
                "(kc p) n -> p kc n", p=128), "s_cast_ff1_0", ("s_ff1", 0)))
            for g in range(1, 8):
                casts.append((s_ff1[g], w_ff1[:, g * 512:(g + 1) * 512].rearrange(
                    "(kc p) n -> p kc n", p=128), "s_cast", ("s_ff1", g)))
            for g in range(2):
                casts.append((s_gate[g], w_gate[:, g * 512:(g + 1) * 512].rearrange(
                    "(kc p) n -> p kc n", p=128), "s_cast", ("s_gate", g)))
            for c in range(8):
                casts.append((s_ff2[c], w_ff2[:, c * 128:(c + 1) * 128].rearrange(
                    "(fc p) n -> p fc n", p=128), "s_cast", ("s_ff2", c)))

            def issue_casts(cnt):
                for _ in range(cnt):
                    if casts:
                        dst, src, key, res = casts.pop(0)
                        h.dma("pool", dst, src, key, [], [res])

            U = [ubuf[j // 2][:, (j % 2) * 514:(j % 2 + 1) * 514] for j in range(4)]
            rot_gen = setup_rotary()
            it = 0
            for tt in range(4):
                for j in range(4):
                    ub = U[j]
                    r_u = ("U", j)
                    if tt == 0:
                        h.P.op("dve", lambda e, ub=ub: e.memset(ub[:, 0:2], 0.0), [], [r_u])
                    par = it % 2
                    it += 1
                    tok = slice(tt * 512, (tt + 1) * 512)
                    r_xT = ("xT", tt)
                    banks = (pf[par], pf[2 + par], pf[4 + par])
                    rb = (PF[par], PF[2 + par], PF[4 + par])
                    for gi, slab in enumerate((wB, wC, wH)):
                        for kc in range(8):
                            h.mm(banks[gi][:], slab[:, kc, j * 128:(j + 1) * 128], xT[:, kc, tok],
                                 kc == 0, kc == 7, [("wslj", gi, j), r_xT], [rb[gi]])
                    hsb = hs[par]
                    ab = abuf[par]
                    h.cp("act", hsb[:], banks[2][:], [rb[2]], [("hs", par)])
                    u_t = ub[:, 2:514]
                    u_m1 = ub[:, 1:513]
                    u_m2 = ub[:, 0:512]
                    h.tt("dve", u_t, banks[1][:], hsb[:], ALU.mult, [rb[1], ("hs", par)], [r_u])
                    h.act(ab[:], u_t, AF.Copy, [r_u, "smallT"], [("ab", par)],
                          scale=cwT[:, 8 + j:9 + j])
                    h.stt(ab[:], u_m1, cwT[:, 4 + j:5 + j], ab[:], ALU.mult, ALU.add,
                          [r_u, ("ab", par), "smallT"], [("ab", par)])
                    h.stt(ab[:], u_m2, cwT[:, j:j + 1], ab[:], ALU.mult, ALU.add,
                          [r_u, ("ab", par), "smallT"], [("ab", par)])
                    h.tt("dve", ycat[:, j, tok], banks[0][:], ab[:], ALU.mult,
                         [rb[0], ("ab", par)], [("ycat", tt)])
                    if tt < 3:
                        h.cp("dve", ub[:, 0:2], ub[:, 512:514], [r_u], [r_u])
                    if tt >= 1:
                        for _ in range(2):
                            next(rot_gen, None)
                        if j % 2 == 1:
                            issue_casts(1)
            for _ in rot_gen:
                pass

            q_ps, k_ps, v_ps, s_ps, o_ps, kv_ps = pf
            rq, rk, rv, rs, ro, rkv = PF

            def ctok_(n):
                return slice(n * 128, (n + 1) * 128)

            def rx(n):
                return ("xT", n // 4)

            def P1(n):
                for (bank, rbk, slab, si) in ((v_ps, rv, wV, 5), (q_ps, rq, wQ, 3), (k_ps, rk, wK, 4)):
                    for kc in range(8):
                        h.mm(bank[:], xT[:, kc, ctok_(n)], slab[:, kc, :], kc == 0, kc == 7,
                             [("wsl", si), rx(n)], [rbk])

            def E1(n):
                cosb = cosF[:, n, :].unsqueeze(1).broadcast_to([128, 4, 128])
                sin_lo = sinS[:, n, 0:64].unsqueeze(1).broadcast_to([128, 4, 64])
                sin_hi = sinS[:, n, 64:128].unsqueeze(1).broadcast_to([128, 4, 64])
                sin_b = sinS[:, n, :].rearrange("p (t d) -> p t d", t=2).unsqueeze(1).broadcast_to(
                    [128, 4, 2, 64])
                h.cp("act", vb[n % 4][:], v_ps[:], [rv], [("vb", n % 4)])
                h.tt("dve", f4(vz[n % 4][:]), f4(v_ps[:]),
                     zeta[:].unsqueeze(2).broadcast_to([128, 4, 128]), ALU.mult,
                     [rv, "zeta"], [("vz", n % 4)])
                for ti, (bank, rbk, dst, dname) in enumerate(
                        ((q_ps, rq, qr[n % 2], ("qr", n % 2)), (k_ps, rk, kr[n % 4], ("kr", n % 4)))):
                    b4 = f4(bank[:])
                    a1, a2 = t1[ti], t2[ti]
                    h.tt("dve", a1, b4, cosb, ALU.mult, [rbk, "cosF"], [("t1", ti)])
                    b5s = bank[:].rearrange("p (h t d) -> p h t d", h=4, t=2)[:, :, ::-1, :]
                    h.tt("dve", a2.rearrange("p h (t d) -> p h t d", t=2), b5s, sin_b, ALU.mult,
                         [rbk, "sinS"], [("t2", ti)])
                    h.tt("pool", f4(dst[:]), a1, a2, ALU.add, [("t1", ti), ("t2", ti)], [dname])

            def P2(n):
                for hh in range(4):
                    h.tr(pb[0][:, hh, :], qr[n % 2][:, hh * 128:(hh + 1) * 128], ident[:],
                         [("qr", n % 2), "ident"], [PB[0]])
                for hh in range(4):
                    h.tr(pb[0][:, 4 + hh, :], kr[n % 4][:, hh * 128:(hh + 1) * 128], ident[:],
                         [("kr", n % 4), "ident"], [PB[0]])

            def E2(n):
                h.tt("dve", qhT[n % 3][:], pb[0][:, 0:4, :], xiT[:], ALU.mult, [PB[0], "xiT"],
                     [("qhT", n % 3)])
                h.cp("act", kT[n % 2][:], pb[0][:, 4:8, :], [PB[0]], [("kT", n % 2)])

            def P3(n):
                for hh in range(4):
                    h.mm(s_ps[:, hh * 128:(hh + 1) * 128], kT[n % 2][:, hh, :], qhT[n % 3][:, hh, :],
                         True, True, [("kT", n % 2), ("qhT", n % 3)], [rs])

            def E3(n):
                h.tt("dve", PT[n % 2][:], f4(s_ps[:]), maskT[:], ALU.mult, [rs, "maskT"],
                     [("PT", n % 2)])

            def P4(n):
                for hh in range(4):
                    hs_ = slice(hh * 128, (hh + 1) * 128)
                    h.mm(o_ps[:, hs_], PT[n % 2][:, hh, :], vb[n % 4][:, hs_], True, n == 0,
                         [("PT", n % 2), ("vb", n % 4)], [ro])
                    if n > 0:
                        h.mm(o_ps[:, hs_], qhT[n % 3][:, hh, :], Sb[n % 2][:, hs_], False, True,
                             [("qhT", n % 3), ("Sb", n % 2)], [ro])
                if n < NB - 1:
                    for hh in range(4):
                        hs_ = slice(hh * 128, (hh + 1) * 128)
                        h.mm(kv_ps[:, hs_], kr[n % 4][:, hs_], vz[n % 4][:, hs_], True, True,
                             [("kr", n % 4), ("vz", n % 4)], [rkv])

            def E4a(n):
                if n < NB - 1:
                    if n == 0:
                        h.cp("dve", Sst[:], kv_ps[:], [rkv], ["Sst"])
                    else:
                        h.tt("dve", Sst[:], kv_ps[:], Stmp[:], ALU.add, [rkv, "Stmp"], ["Sst"])
                    h.cp("act", Sb[(n + 1) % 2][:], Sst[:], ["Sst"], [("Sb", (n + 1) % 2)])
                    if n + 1 < NB - 1:
                        h.tt("pool", Stmp[:], Sst[:], cdT[:], ALU.mult, ["Sst", "cdT"], ["Stmp"])

            def P4g(n):
                for hh in range(4):
                    for kc in range(8):
                        h.mm(v_ps[:, hh * 128:(hh + 1) * 128], wG[:, kc, hh * 128:(hh + 1) * 128],
                             xT[:, kc, ctok_(n)], kc == 0, kc == 7, [("wsl", 6), rx(n)], [rv])
                h.act(gsr[n % 2], f4(v_ps[:]), AF.Silu, [rv], [("gsr", n % 2)])

            def E4b(n):
                i = n % 2
                sum4 = st4[:, i, 0:4]
                ssq4 = st4[:, i, 4:8]
                mean4 = st4[:, i, 8:12]
                m24 = st4[:, i, 12:16]
                var4 = st4[:, i, 16:20]
                rstd4 = st4[:, i, 20:24]
                nb4 = st4[:, i, 24:28]
                rs4 = ("st4", i)
                o4 = f4(o_ps[:])
                rsa = ("st4a", i)
                rsb = ("st4b", i)
                h.red(sum4, o4, [ro], [rsa])
                h.act(sq, o4, AF.Square, [ro], ["sq"])
                h.red(ssq4, sq, ["sq"], [rsb])
                h.ts("pool", mean4, sum4, 1.0 / 128.0, ALU.mult, [rsa], [rs4], s2=0.0, op1=ALU.add)
                h.tt("pool", m24, mean4, mean4, ALU.mult, [rs4], [rs4])
                h.ts("pool", var4, ssq4, 1.0 / 128.0, ALU.mult, [rsb, rs4], [rs4], s2=GN_EPS,
                     op1=ALU.add)
                h.tt("pool", var4, var4, m24, ALU.subtract, [rs4], [rs4])
                h.tt("pool", rstd4, var4, neghalf[:], ALU.pow, [rs4, "neghalf"], [rs4])
                h.tt("dve", oc, o4, mean4.unsqueeze(2).broadcast_to([128, 4, 128]), ALU.subtract,
                     [ro, rs4], ["oc"])
                h.tt("dve", f4(ob[i][:]), oc, rstd4.unsqueeze(2).broadcast_to([128, 4, 128]),
                     ALU.mult, ["oc", rs4], [("ob", i)])

            def P5(n):
                for hh in range(4):
                    h.tr(pb[1][:, hh, :], ob[n % 2][:, hh * 128:(hh + 1) * 128], ident[:],
                         [("ob", n % 2), "ident"], [PB[1]])

            def E5(n):
                i = n % 2
                for hh in range(4):
                    h.act(yn[i][:, hh, :], pb[1][:, hh, :], AF.Identity, [PB[1], "smallT"],
                          [("yn", i)], scale=gnwT[:, hh:hh + 1], bias=gnbT[:, hh:hh + 1])
                h.tt("pool", ycat[:, 4:8, ctok_(n)], yn[i], gsr[i], ALU.mult,
                     [("yn", i), ("gsr", i)], [("ycat", n // 4)])

            def ok(n):
                return 0 <= n < NB

            gen_lnA = make_gen_ln(P, h, lnstA, neghalf)
            ln1wA = lnA[:, 0:1024]
            ln1bA = lnA[:, 1024:2048]

            def b0_setup():
                h.dma("sp", wsl[3][:], s_wout[0], ("b0k", 0), [("s_wout", 0)], [("wsl", 3)])
                h.dma("sp", wsl[4][:], s_wout[1], ("b0k", 1), [("s_wout", 1)], [("wsl", 4)])
                h.dma("sp", ln1wA, ln1_w.partition_broadcast(128), ("b0l", 0), [], ["ln1wA"])
                h.dma("sp", ln1bA, ln1_b.partition_broadcast(128), ("b0l", 1), [], ["ln1bA"])

            def b0_mix(bi):
                h.dma("sp", xinA[bi % 2], x[bi * 128:(bi + 1) * 128, :], ("xinA", bi % 2),
                      [], [("xinA", bi % 2)])
                bank, rbk = pf[bi % 2], PF[bi % 2]
                for hf in range(2):
                    for kc in range(8):
                        h.mm(bank[:], ycat[:, kc, bi * 128:(bi + 1) * 128], wsl[3 + hf][:, kc, :],
                             kc == 0, kc == 7, [("ycat", 0), ("wsl", 3 + hf)], [rbk])
                    h.stt(x1v(bi)[:, hf * 512:(hf + 1) * 512], xinA[bi % 2][:, hf * 512:(hf + 1) * 512],
                          ALPHA, bank[:], ALU.mult, ALU.add, [("xinA", bi % 2), rbk], [("x1", bi)])

            def gen_b0norm(bi):
                yield from gen_lnA(x1v(bi), ("x1", bi), ln1wA, ln1bA, "ln1wA", "ln1bA", bi)
                h.cp("act", x1bA[bi], x1v(bi), [("x1", bi)], [("x1bA", bi)])
                yield
                yield
                yield
                for kc in range(8):
                    h.tr(pb[0][:, kc, :], x1bA[bi][:, kc * 128:(kc + 1) * 128], ident[:],
                         [("x1bA", bi), "ident"], [PB[0]])
                h.cp("act", ycat[:, :, bi * 128:(bi + 1) * 128], pb[0][:], [PB[0]], [("ycat", 0)])
                yield

            b0_gens = []

            def b0_advance(k):
                for _ in range(k):
                    for g in list(b0_gens):
                        try:
                            next(g)
                        except StopIteration:
                            b0_gens.remove(g)

            P1(0)
            for t in range(NB + 4):
                drain = t >= 15
                if ok(t):
                    E1(t)
                if drain:
                    b0_advance(1)
                if ok(t - 1):
                    P2(t - 1)
                    E2(t - 1)
                if drain:
                    b0_advance(1)
                if ok(t - 2):
                    P3(t - 2)
                    E3(t - 2)
                if drain:
                    b0_advance(1)
                if ok(t - 3):
                    P4g(t - 3)
                    P4(t - 3)
                    E4a(t - 3)
                    E4b(t - 3)
                if drain:
                    b0_advance(2)
                if ok(t - 4):
                    P5(t - 4)
                    E5(t - 4)
                if ok(t + 1):
                    P1(t + 1)
                issue_casts(1)
                if t == 15:
                    b0_setup()
                if t == 17:
                    h.dma("sp", kpre[:], s_ff1[0], "kpre", [("s_ff1", 0)], ["kpre"])
                if t in (15, 16):
                    for bi in (2 * (t - 15), 2 * (t - 15) + 1):
                        b0_mix(bi)
                        b0_gens.append(gen_b0norm(bi))
                if drain:
                    b0_advance(3)
            issue_casts(len(casts))
            while b0_gens:
                b0_advance(1)

            P.join_all()
            P.emit(st, "a", st0)

        with ExitStack() as st:
            def sb(name, shape, dt):
                return st.enter_context(nc.sbuf_tensor(name, shape, dt))

            def psf(name):
                return st.enter_context(nc.psum_tensor(name, [128, 512], F32))

            def psb(name):
                return st.enter_context(nc.psum_tensor(name, [128, 8, 128], BF16))

            P = Prog(nc)
            h = H(P)

            lnt = [sb("lnt%d" % i, [128, D], F32) for i in range(4)]
            ln1w_t, ln1b_t, ln2w_t, ln2b_t = lnt
            wproj = sb("wproj", [128, 2, D], BF16)
            acc2 = sb("acc2", [128, 4, D], F32)
            hT = sb("hT", [128, 32, 512], BF16)
            NK = 3
            NC2 = 3
            kslab = [sb("kslab%d" % i, [128, 8, 512], BF16) for i in range(NK)]
            cslab = [sb("cslab%d" % i, [128, 32, 128], BF16) for i in range(NC2)]
            xin = [sb("xin%d" % i, [128, D], F32) for i in range(2)]
            x1b = [sb("x1b%d" % i, [128, D], BF16) for i in range(4)]
            pbf = [sb("pbf%d" % i, [128, 256], BF16) for i in range(4)]
            pT = sb("pT", [128, 4, 2, 128], BF16)
            sg = [sb("sg%d" % i, [128, 512], F32) for i in range(2)]
            tR = [sb("tR%d" % i, [128, 512], F32) for i in range(2)]
            ffs = [sb("ffs%d" % i, [128, 512], F32) for i in range(2)]
            lnst = sb("lnst", [128, 4, 16], F32)

            pm = [psf("pm%d" % i) for i in range(4)]
            pF = [psf("pF%d" % i) for i in range(2)]
            pT16 = psb("pT16")
            pT32 = psf("pT32")
            RM = [("ps", i) for i in range(4)]
            RF = [("ps", 4 + i) for i in range(2)]
            RT16 = ("ps", 6)
            RT32 = ("ps", 7)


            kcnt = [0]
            ccnt = [0]

            def load_k(src):
                i = kcnt[0] % NK
                kcnt[0] += 1
                h.dma("sp", kslab[i][:], src, ("kslab", i), [], [("kslab", i)])
                return kslab[i], ("kslab", i)

            def load_c(src):
                i = ccnt[0] % NC2
                ccnt[0] += 1
                h.dma("sp", cslab[i][:], src, ("cslab", i), [], [("cslab", i)])
                return cslab[i], ("cslab", i)

            fcnt = [0]

            def next_F():
                i = fcnt[0] % 2
                fcnt[0] += 1
                return pF[i], RF[i], i

            gen_ln = make_gen_ln(P, h, lnst, neghalf)

            def drain(g):
                for _ in g:
                    pass

            def drain_rr(gens):
                gens = list(gens)
                while gens:
                    for g in list(gens):
                        try:
                            next(g)
                        except StopIteration:
                            gens.remove(g)

            bgq = []

            def bg_step(k=1):
                for _ in range(k):
                    while bgq:
                        try:
                            next(bgq[0])
                            break
                        except StopIteration:
                            bgq.pop(0)

            def bg_flush():
                while bgq:
                    drain(bgq.pop(0))

            def B_mix_block(tt, bi, slabs):
                blk = tt * 4 + bi
                h.dma("sp", xin[bi % 2][:], x[blk * 128:(blk + 1) * 128, :], ("xin", bi % 2),
                      [], [("xin", bi % 2)])
                for hf in range(2):
                    slab, rsl = slabs[hf]
                    bk = (2 * bi + hf) % 4
                    for kc in range(8):
                        h.mm(pm[bk][:], ycat[:, kc, blk * 128:(blk + 1) * 128], slab[:, kc, :],
                             kc == 0, kc == 7, [("ycat", tt), rsl], [RM[bk]])
                    h.stt(x1v(bi)[:, hf * 512:(hf + 1) * 512], xin[bi % 2][:, hf * 512:(hf + 1) * 512],
                          ALPHA, pm[bk][:], ALU.mult, ALU.add, [("xin", bi % 2), RM[bk]],
                          [("x1", bi)])

            def phase_B_mix(tt):
                slabs = [load_k(s_wout[0]), load_k(s_wout[1])]
                for bi in range(4):
                    B_mix_block(tt, bi, slabs)

            def gen_bnorm(tt, bi):
                yield from gen_ln(x1v(bi), ("x1", bi), ln1w_t[:], ln1b_t[:], "ln1w", "ln1b", bi)
                h.cp("act", x1b[bi][:], x1v(bi), [("x1", bi)], [("x1b", bi)])
                yield
                yield
                yield
                for kc in range(8):
                    h.tr(pT16[:, kc, :], x1b[bi][:, kc * 128:(kc + 1) * 128], ident[:],
                         [("x1b", bi), "ident"], [RT16])
                h.cp("act", ycat[:, :, (tt * 4 + bi) * 128:(tt * 4 + bi + 1) * 128], pT16[:],
                     [RT16], [("ycat", tt)])
                yield

            def phase_B_norm(tt):
                drain_rr(gen_bnorm(tt, bi) for bi in range(4))

            def phase_ff1(tt):
                tok = slice(tt * 512, (tt + 1) * 512)
                for g in range(8):
                    if tt == 0 and g == 0:
                        slab, rsl = kpre, "kpre"
                    else:
                        slab, rsl = load_k(s_ff1[g])
                    for f4 in range(4):
                        fc = g * 4 + f4
                        bank, rbk, i = next_F()
                        for kc in range(8):
                            h.mm(bank[:], slab[:, kc, f4 * 128:(f4 + 1) * 128], ycat[:, kc, tok],
                                 kc == 0, kc == 7, [rsl, ("ycat", tt)], [rbk])
                        h.act(tR[i][:], bank[:], AF.Relu, [rbk], [("tR", i)])
                        h.tt("dve", hT[:, fc, :], bank[:], tR[i][:], ALU.mult, [rbk, ("tR", i)],
                             [("hT", fc)])
                        bg_step(2)

            def load_p(tt, bi):
                blk = tt * 4 + bi
                h.dma("pool", pbf[bi][:], p_in[blk * 128:(blk + 1) * 128, :], ("pbf", bi),
                      [], [("pbf", bi)])

            def phase_gate_ple(tt):
                for bi in range(4):
                    for k2 in range(2):
                        h.tr(pT16[:, k2, :], pbf[bi][:, k2 * 128:(k2 + 1) * 128], ident[:],
                             [("pbf", bi), "ident"], [RT16])
                    h.cp("act", pT[:, bi, :, :], pT16[:, 0:2, :], [RT16], [("pT", bi)])
                slabs = [load_k(s_gate[0]), load_k(s_gate[1])]
                cnt = 0
                for hf in range(2):
                    slab, rsl = slabs[hf]
                    cols = slice(hf * 512, (hf + 1) * 512)
                    for bi in range(4):
                        for kc in range(8):
                            h.mm(pm[bi][:], ycat[:, kc, (tt * 4 + bi) * 128:(tt * 4 + bi + 1) * 128],
                                 slab[:, kc, :], kc == 0, kc == 7, [("ycat", tt), rsl], [RM[bi]])
                        i = cnt % 2
                        cnt += 1
                        h.act(sg[i][:], pm[bi][:], AF.Sigmoid, [RM[bi]], [("sg", i)])
                        bank, rbk, _ = next_F()
                        for k2 in range(2):
                            h.mm(bank[:], pT[:, bi, k2, :], wproj[:, k2, cols], k2 == 0, k2 == 1,
                                 [("pT", bi), "wproj"], [rbk])
                        h.tt("dve", acc2[:, bi, cols], bank[:], sg[i][:], ALU.mult,
                             [rbk, ("sg", i)], [("acc2", bi)])
                        h.stt(acc2[:, bi, cols], x1v(bi)[:, cols], ALPHA, acc2[:, bi, cols],
                              ALU.mult, ALU.add, [("x1", bi), ("acc2", bi)], [("acc2", bi)])

            def phase_ff2(tt):
                a4 = acc2[:]
                pend = None

                def fin(c, i):
                    for bi in range(4):
                        h.tr(pT32[:, bi * 128:(bi + 1) * 128], ffs[i][:, bi * 128:(bi + 1) * 128],
                             identf[:], [("ffs", i), "identf"], [RT32])
                    h.tt("dve", a4[:, :, c * 128:(c + 1) * 128],
                         pT32[:].rearrange("p (b d) -> p b d", b=4), a4[:, :, c * 128:(c + 1) * 128],
                         ALU.add, [RT32] + [("acc2", b) for b in range(4)],
                         [("acc2", b) for b in range(4)])

                for c in range(8):
                    slab, rsl = load_c(s_ff2[c])
                    bank, rbk, i = next_F()
                    for fc in range(32):
                        h.mm(bank[:], slab[:, fc, :], hT[:, fc, :], fc == 0, fc == 31,
                             [rsl, ("hT", fc)], [rbk])
                        if fc % 8 == 7:
                            bg_step(3)
                        if fc == 7 and pend is not None:
                            fin(*pend)
                            pend = None
                    h.cp("act", ffs[i][:], bank[:], [rbk], [("ffs", i)])
                    pend = (c, i)
                fin(*pend)

            def gen_out(tt, bi):
                blk = tt * 4 + bi
                yield from gen_ln(acc2[:, bi, :], ("acc2", bi), ln2w_t[:], ln2b_t[:], "ln2w", "ln2b", bi)
                h.dma("pool", out[blk * 128:(blk + 1) * 128, :], acc2[:, bi, :], ("outq", bi),
                      [("acc2", bi)], [("out", blk)])
                yield

            h.dma("act", ln1w_t[:], ln1_w.partition_broadcast(128), "lnt0", [], ["ln1w"])
            h.dma("act", ln1b_t[:], ln1_b.partition_broadcast(128), "lnt1", [], ["ln1b"])
            h.dma("pool", wproj[:], w_proj.rearrange("(kc p) n -> p kc n", p=128), "wproj", [],
                  ["wproj"])
            h.dma("act", ln2w_t[:], ln2_w.partition_broadcast(128), "lnt2", [], ["ln2w"])
            h.dma("act", ln2b_t[:], ln2_b.partition_broadcast(128), "lnt3", [], ["ln2b"])
            for tt in range(4):
                if tt > 0:
                    for bi in range(4):
                        bgq.append(gen_out(tt - 1, bi))
                for bi in range(4):
                    load_p(tt, bi)
                phase_ff1(tt)
                bg_flush()
                phase_gate_ple(tt)
                if tt < 3:
                    phase_B_mix(tt + 1)
                    for bi in range(4):
                        bgq.append(gen_bnorm(tt + 1, bi))
                phase_ff2(tt)
                bg_flush()
            drain_rr(gen_out(3, bi) for bi in range(4))
            P.join_all()
            P.emit(st, "b", st0)
    return nc


def _consts():
    hh = np.arange(4, dtype=np.float64)
    log_g = np.log1p(-np.exp2(-5.0 - hh))
    idx = np.arange(128, dtype=np.float64)
    invf = (10000.0 ** (-np.arange(64, dtype=np.float32) * 2.0 / 128)).astype(np.float32)
    xi = np.exp((idx + 1.0)[None, :] * log_g[:, None])
    mj = np.exp(-(idx + 1.0)[:, None] * log_g[None, :]) * (128.0 ** -0.5)
    causal = (idx[None, :] >= idx[:, None]).astype(np.float64)
    mask = mj[:, :, None] * causal[:, None, :]
    zeta = np.exp((127.0 - idx)[:, None] * log_g[None, :]) * (128.0 ** -0.5)
    cd = np.repeat(np.exp(128.0 * log_g), 128)
    return dict(
        c_invf=invf,
        c_xi=xi.reshape(512).astype(np.float32),
        c_mask=mask.reshape(128, 512).astype(np.float32),
        c_zeta=zeta.astype(np.float32),
        c_cd=cd.astype(np.float32),
    )


_NC_CACHE = {}


def kernel(x, p, positions, w_in, conv_w, ret_gn_w, ret_gn_b, w_out, ln1_w, ln1_b,
           w_ff1, w_ff2, w_ple_gate, w_ple_proj, ln2_w, ln2_b):
    n = 8
    f = lambda a: np.ascontiguousarray(np.asarray(a))
    x = f(x); p = f(p); positions = f(positions)
    shared = dict(
        w_in=f(w_in)[0], conv_w=f(conv_w)[0], gn_w=f(ret_gn_w)[0], gn_b=f(ret_gn_b)[0],
        w_out=f(w_out)[0], ln1_w=f(ln1_w)[0], ln1_b=f(ln1_b)[0], w_ff1=f(w_ff1)[0],
        w_ff2=f(w_ff2)[0], w_gate=f(w_ple_gate)[0], w_proj=f(w_ple_proj)[0],
        ln2_w=f(ln2_w)[0], ln2_b=f(ln2_b)[0],
    )
    shared.update(_consts())
    if "nc" not in _NC_CACHE:
        _NC_CACHE["nc"] = build_nc()
    nc = _NC_CACHE["nc"]
    in_maps = []
    for b in range(n):
        m = dict(shared)
        m["x"] = f(x[b])
        m["p"] = f(p[0, b])
        m["pos"] = f(positions[b]).astype(np.int32)
        in_maps.append(m)
    res = run_bass_kernel_spmd(nc, in_maps, core_ids=list(range(n)))
    return np.stack([np.asarray(r["out"]) for r in res.results], axis=0).astype(np.float32)
```
